# Optimizing a Trainium2 kernel written in Bass

```python
import jax, jax.numpy as jnp
from jax import lax
import numpy as np

D_MODEL = 2048
BATCH = 2
SEQ = 16384
DEPTH = 1
DEC_BATCH = 8
DEC_SEQ = 32
PAST_LEN = 2048

CHUNK = 64
D_CONV_A = D_MODEL // 2
D_CONV_B = D_MODEL // 2
CONV_A_WIDTH = 31
CONV_B_WIDTH = 3
N_MEM = 256
N_XHEADS = 4
XHEAD_DIM = D_MODEL // N_XHEADS
D_FF = -(-(8 * D_MODEL) // (3 * 256)) * 256
RMS_EPS = 1e-6
LN_EPS = 1e-5
IN_SIZES = (D_CONV_A, D_CONV_A, D_CONV_B, D_CONV_B, D_CONV_B, D_MODEL, D_MODEL)
IN_SPLITS = tuple(int(v) for v in np.cumsum(IN_SIZES)[:-1])
D_IN = int(sum(IN_SIZES))

kernel_name = "gated_conv_hybrid_stream_step"


def _rms_norm(x, g):
    x32 = x.astype(jnp.float32)
    y = x32 * lax.rsqrt(jnp.mean(jnp.square(x32), axis=-1, keepdims=True) + RMS_EPS)
    return y.astype(x.dtype) * g


def _layer_norm(x, g, b):
    x32 = x.astype(jnp.float32)
    mu = jnp.mean(x32, axis=-1, keepdims=True)
    xc = x32 - mu
    var = jnp.mean(jnp.square(xc), axis=-1, keepdims=True)
    return (xc * lax.rsqrt(var + LN_EPS)).astype(x.dtype) * g + b


def _causal_dwconv(u, ctx, w):
    k = w.shape[0]
    full = jnp.concatenate([ctx.astype(u.dtype), u], axis=1)
    out = lax.conv_general_dilated(full, w[:, None, :].astype(u.dtype), window_strides=(1,),
                                   padding='VALID', dimension_numbers=('NWC', 'WIO', 'NWC'),
                                   feature_group_count=u.shape[-1])
    return out, full[:, -(k - 1):, :]


def _memory_kv(mem, g_mem, w_k, w_v):
    b = mem.shape[0]
    mn = _rms_norm(mem, g_mem)
    k = (mn @ w_k).reshape(b, N_MEM, N_XHEADS, XHEAD_DIM)
    v = (mn @ w_v).reshape(b, N_MEM, N_XHEADS, XHEAD_DIM)
    return k, v


def _layer(x, ctx_a, ctx_b, mem_k, mem_v, lw):
    b, t, _ = x.shape
    u = _rms_norm(x, lw['norm_mix_g'])
    z = u @ lw['w_in'] + lw['b_in']
    a_val, a_gate, b_b, b_c, b_x, g_a, g_b = jnp.split(z, IN_SPLITS, axis=-1)
    a = a_val * jax.nn.sigmoid(a_gate)
    a, new_a = _causal_dwconv(a, ctx_a, lw['conv_a_w'])
    a = _layer_norm(a + lw['conv_a_b'], lw['ln_a_g'], lw['ln_a_b'])
    a = jax.nn.silu(a) @ lw['w_a_out'] + lw['b_a_out']
    cb, new_b = _causal_dwconv(b_c * b_x, ctx_b, lw['conv_b_w'])
    bb = (b_b * cb) @ lw['w_b_out']
    m = jax.nn.sigmoid(g_a) * a + jax.nn.sigmoid(g_b) * bb
    x = x + m @ lw['w_mix_out']
    q = (_rms_norm(x, lw['norm_x_g']) @ lw['w_q']).reshape(b, t, N_XHEADS, XHEAD_DIM)
    s = jnp.einsum('bthd,bmhd->bhtm', q.astype(jnp.float32), mem_k.astype(jnp.float32)) * (XHEAD_DIM ** -0.5)
    p = jax.nn.softmax(s, axis=-1)
    o = jnp.einsum('bhtm,bmhd->bthd', p, mem_v.astype(jnp.float32)).astype(x.dtype).reshape(b, t, D_MODEL)
    x = x + o @ lw['w_o']
    h = _rms_norm(x, lw['norm_ffn_g'])
    x = x + (jax.nn.silu(h @ lw['w_gate']) * (h @ lw['w_up'])) @ lw['w_down']
    return x, new_a, new_b


def setup_inputs(seed: int = 0) -> dict:
    key = jax.random.key(seed)
    ks = jax.random.split(key, 30)
    f32 = jnp.float32

    def nrm(k, shape, scale):
        return jax.random.normal(k, shape, f32) * scale

    def gain(k, shape):
        return 1.0 + 0.01 * jax.random.normal(k, shape, f32)

    L = DEPTH
    return {
        'x_prompt': nrm(ks[0], (BATCH, SEQ, D_MODEL), 1.0),
        'x_sample': nrm(ks[1], (DEC_BATCH, DEC_SEQ, D_MODEL), 1.0),
        'state_conv_a': nrm(ks[2], (L, DEC_BATCH, CONV_A_WIDTH - 1, D_CONV_A), 0.5),
        'state_conv_b': nrm(ks[3], (L, DEC_BATCH, CONV_B_WIDTH - 1, D_CONV_B), 0.5),
        'cache_mem_k': nrm(ks[4], (L, DEC_BATCH, N_MEM, N_XHEADS, XHEAD_DIM), 1.0),
        'cache_mem_v': nrm(ks[5], (L, DEC_BATCH, N_MEM, N_XHEADS, XHEAD_DIM), 1.0),
        'mem_prompt': nrm(ks[6], (BATCH, N_MEM, D_MODEL), 1.0),
        'norm_mix_g': gain(ks[7], (L, D_MODEL)),
        'w_in': nrm(ks[8], (L, D_MODEL, D_IN), D_MODEL ** -0.5),
        'b_in': nrm(ks[9], (L, D_IN), 0.01),
        'conv_a_w': nrm(ks[10], (L, CONV_A_WIDTH, D_CONV_A), CONV_A_WIDTH ** -0.5),
        'conv_a_b': nrm(ks[11], (L, D_CONV_A), 0.01),
        'ln_a_g': gain(ks[12], (L, D_CONV_A)),
        'ln_a_b': nrm(ks[13], (L, D_CONV_A), 0.01),
        'w_a_out': nrm(ks[14], (L, D_CONV_A, D_MODEL), D_CONV_A ** -0.5),
        'b_a_out': nrm(ks[15], (L, D_MODEL), 0.01),
        'conv_b_w': nrm(ks[16], (L, CONV_B_WIDTH, D_CONV_B), CONV_B_WIDTH ** -0.5),
        'w_b_out': nrm(ks[17], (L, D_CONV_B, D_MODEL), D_CONV_B ** -0.5),
        'w_mix_out': nrm(ks[18], (L, D_MODEL, D_MODEL), D_MODEL ** -0.5),
        'norm_x_g': gain(ks[19], (L, D_MODEL)),
        'norm_mem_g': gain(ks[20], (L, D_MODEL)),
        'w_q': nrm(ks[21], (L, D_MODEL, D_MODEL), D_MODEL ** -0.5),
        'w_k': nrm(ks[22], (L, D_MODEL, D_MODEL), D_MODEL ** -0.5),
        'w_v': nrm(ks[23], (L, D_MODEL, D_MODEL), D_MODEL ** -0.5),
        'w_o': nrm(ks[24], (L, D_MODEL, D_MODEL), D_MODEL ** -0.5),
        'norm_ffn_g': gain(ks[25], (L, D_MODEL)),
        'w_gate': nrm(ks[26], (L, D_MODEL, D_FF), D_MODEL ** -0.5),
        'w_up': nrm(ks[27], (L, D_MODEL, D_FF), D_MODEL ** -0.5),
        'w_down': nrm(ks[28], (L, D_FF, D_MODEL), D_FF ** -0.5),
        'norm_final_g': gain(ks[29], (D_MODEL,)),
    }


def reference(x_prompt, x_sample, state_conv_a, state_conv_b, cache_mem_k, cache_mem_v, mem_prompt,
              norm_mix_g, w_in, b_in, conv_a_w, conv_a_b, ln_a_g, ln_a_b, w_a_out, b_a_out,
              conv_b_w, w_b_out, w_mix_out, norm_x_g, norm_mem_g, w_q, w_k, w_v, w_o,
              norm_ffn_g, w_gate, w_up, w_down, norm_final_g):
    hp, hs = x_prompt, x_sample
    conv_a_p, conv_b_p, mem_k_p, mem_v_p, conv_a_s, conv_b_s = [], [], [], [], [], []
    for l in range(DEPTH):
        lw = {
            'norm_mix_g': norm_mix_g[l], 'w_in': w_in[l], 'b_in': b_in[l],
            'conv_a_w': conv_a_w[l], 'conv_a_b': conv_a_b[l], 'ln_a_g': ln_a_g[l], 'ln_a_b': ln_a_b[l],
            'w_a_out': w_a_out[l], 'b_a_out': b_a_out[l], 'conv_b_w': conv_b_w[l], 'w_b_out': w_b_out[l],
            'w_mix_out': w_mix_out[l], 'norm_x_g': norm_x_g[l], 'w_q': w_q[l], 'w_o': w_o[l],
            'norm_ffn_g': norm_ffn_g[l], 'w_gate': w_gate[l], 'w_up': w_up[l], 'w_down': w_down[l],
        }
        kp, vp = _memory_kv(mem_prompt, norm_mem_g[l], w_k[l], w_v[l])
        zero_a = jnp.zeros((hp.shape[0], CONV_A_WIDTH - 1, D_CONV_A), hp.dtype)
        zero_b = jnp.zeros((hp.shape[0], CONV_B_WIDTH - 1, D_CONV_B), hp.dtype)
        hp, na_p, nb_p = _layer(hp, zero_a, zero_b, kp, vp, lw)
        hs, na_s, nb_s = _layer(hs, state_conv_a[l], state_conv_b[l], cache_mem_k[l], cache_mem_v[l], lw)
        conv_a_p.append(na_p); conv_b_p.append(nb_p); mem_k_p.append(kp); mem_v_p.append(vp)
        conv_a_s.append(na_s); conv_b_s.append(nb_s)
    y_prompt = _rms_norm(hp, norm_final_g)
    y_sample = _rms_norm(hs, norm_final_g)
    new_conv_a_prompt = jnp.stack(conv_a_p)
    new_conv_b_prompt = jnp.stack(conv_b_p)
    new_mem_k_prompt = jnp.stack(mem_k_p)
    new_mem_v_prompt = jnp.stack(mem_v_p)
    new_conv_a_sample = jnp.stack(conv_a_s)
    new_conv_b_sample = jnp.stack(conv_b_s)
    return (y_prompt, y_sample, new_conv_a_prompt, new_conv_b_prompt, new_mem_k_prompt, new_mem_v_prompt, new_conv_a_sample, new_conv_b_sample)
```

```python
import numpy as np
import concourse.bass as bass
import concourse.mybir as mybir
from concourse.bass_utils import run_bass_kernel_spmd

F32 = mybir.dt.float32
BF16 = mybir.dt.bfloat16
AF = mybir.ActivationFunctionType
ALU = mybir.AluOpType

RMS_EPS = 1e-6
LN_EPS = 1e-5
NCORES = 8


class Cfg:
    def __init__(self, D=2048, DFF=5632, NMEM=256, NH=4, NP=4096, TW=464, KA=31, KB=3,
                 NS=32, HALO=32, NSLOT=14):
        self.D, self.DFF, self.NMEM, self.NH, self.NP, self.TW = D, DFF, NMEM, NH, NP, TW
        self.KA, self.KB, self.NS, self.HALO, self.NSLOT = KA, KB, NS, HALO, NSLOT
        self.DC = D // 2
        self.ND, self.NCC, self.NF, self.NM = D // 128, self.DC // 128, DFF // 128, NMEM // 128
        self.HD = D // NH
        self.HC = self.HD // 128
        self.DIN = 5 * self.DC + 2 * D
        self.NIN = self.DIN // 128
        n = self.NCC
        self.C_AV, self.C_AG, self.C_BB, self.C_BC, self.C_BX = 0, n, 2 * n, 3 * n, 4 * n
        self.C_GA, self.C_GB = 5 * n, 5 * n + self.ND
        self.PA, self.PB = KA - 1, KB - 1
        o = 0
        def take(k):
            nonlocal o
            r = o
            o += k
            return r
        self.V_NMIX = take(self.ND)
        self.V_BIN = take(self.NIN)
        self.V_CAW = take(KA * n)
        self.V_CAB = take(n)
        self.V_LNG = take(n)
        self.V_LNB = take(n)
        self.V_BAO = take(self.ND)
        self.V_CBW = take(KB * n)
        self.V_NX = take(self.ND)
        self.V_NMEM = take(self.ND)
        self.V_NFFN = take(self.ND)
        self.V_NFIN = take(self.ND)
        self.NV = o
        self.weights = [
            ("w_in", D, self.DIN), ("w_a_out", self.DC, D), ("w_b_out", self.DC, D),
            ("w_mix_out", D, D), ("w_q", D, D), ("w_k", D, D), ("w_v", D, D), ("w_o", D, D),
            ("w_gate", D, DFF), ("w_up", D, DFF), ("w_down", DFF, D),
        ]
        self.wtile = {}
        self.units = []
        off = 0
        for name, K, N in self.weights:
            nkc, njo = K // 128, N // 128
            kgs = [(k0, min(16, nkc - k0)) for k0 in range(0, nkc, 16)]
            for kg, (k0, nk) in enumerate(kgs):
                for jo0 in range(0, njo, 4):
                    nj = min(4, njo - jo0)
                    self.units.append((name, k0, nk, jo0, nj, off))
                    for j in range(nj):
                        self.wtile[(name, jo0 + j, kg)] = (off, nk, k0)
                        off += 128 * nk * 128
        self.WSCR = off
        self.nkg = {name: (K // 128 + 15) // 16 for name, K, N in self.weights}
        tot = HALO + NP
        nt = -(-tot // TW)
        base = -(-tot // nt)
        base = -(-base // 8) * 8
        widths = []
        rem = tot
        while rem > 0:
            w = min(base, rem)
            widths.append(w)
            rem -= w
        self.tiles = []
        r = 0
        for i, w in enumerate(widths):
            h = HALO if i == 0 else 0
            self.tiles.append(dict(W=w, row0=r, kind="prompt", outs=[(h, w - h, r + h - HALO)], halo=h))
            r += w
        self.tiles.append(dict(W=NS, row0=HALO + NP, kind="sample", outs=[(0, NS, NP)], halo=0))
        self.NROWS = HALO + NP + NS


class _Stop(Exception):
    pass


class Ev:
    __slots__ = ("sem", "val")

    def __init__(self, sem, val):
        self.sem, self.val = sem, val


class Buf:
    __slots__ = ("name", "w", "r", "lo", "hi", "al", "excl")

    def __init__(self, name, lo=None, n=None):
        self.name = name
        self.excl = False
        self.w = []
        self.r = {}
        self.lo = lo
        self.hi = None if lo is None else lo + n
        self.al = ()


def link_aliases(bufs):
    rb = [b for b in bufs if b.lo is not None]
    for b in rb:
        b.al = tuple(o for o in rb if o is not b and o.lo < b.hi and b.lo < o.hi)


class Prog:
    ENGS = ("pe", "act", "dve", "pool", "sp")

    def __init__(self):
        self.q = {e: [] for e in self.ENGS}
        self.cnt = {e: 0 for e in self.ENGS}
        self.known = {e: {} for e in self.ENGS}
        self.dcnt = {}

    def _waits(self, eng, reads, writes):
        need = {}
        def add(ev, src, raw):
            if src == eng and (eng == "pe" or not raw):
                return
            if need.get(ev.sem, 0) < ev.val:
                need[ev.sem] = ev.val
        for b in reads:
            for ev, src in b.w:
                add(ev, src, True)
            if b.excl:
                for ev, src in b.r.values():
                    add(ev, src, False)
        for b0 in writes:
            for b in (b0,) + tuple(b0.al):
                for ev, src in b.w:
                    add(ev, src, False)
                for ev, src in b.r.values():
                    add(ev, src, False)
        kn = self.known[eng]
        for sem, val in need.items():
            if kn.get(sem, 0) < val:
                kn[sem] = val
                self.q[eng].append(("wait", sem, val))

    def _commit(self, ev, src, reads, writes):
        for b0 in writes:
            for b in (b0,) + tuple(b0.al):
                b.w = [(ev, src)]
                b.r = {}
        for b in reads:
            old = b.r.get(ev.sem)
            if old is None or old[0].val < ev.val:
                b.r[ev.sem] = (ev, src)

    def op(self, eng, fn, reads=(), writes=(), sig=True):
        self._waits(eng, reads, writes)
        if sig:
            self.cnt[eng] += 1
            ev = Ev("c_" + eng, self.cnt[eng])
            self.q[eng].append(("op", fn, "c_" + eng, 1))
        else:
            ev = Ev("c_" + eng, self.cnt[eng] + 1)
            self.q[eng].append(("op", fn, None, 0))
        self._commit(ev, eng, reads, writes)
        return ev

    def dma(self, eng, sem, fn, reads=(), writes=()):
        self._waits(eng, reads, writes)
        self.dcnt[sem] = self.dcnt.get(sem, 0) + 16
        ev = Ev(sem, self.dcnt[sem])
        self.q[eng].append(("op", fn, sem, 16))
        self._commit(ev, "dma", reads, writes)
        return ev

    def wait_all(self, eng, sems):
        for s in sems:
            v = self.dcnt.get(s, 0)
            if v and self.known[eng].get(s, 0) < v:
                self.known[eng][s] = v
                self.q[eng].append(("wait", s, v))


def build_program(cfg):
    c = cfg
    nc = bass.Bass("TRN2", target_bir_lowering=False)
    D, DC, ND, NCC, NF, NM = c.D, c.DC, c.ND, c.NCC, c.NF, c.NM
    TWB = -(-max([t["W"] for t in c.tiles] + [c.NMEM]) // 16) * 16
    NXIN, NT32, AHEAD = 3, getattr(cfg, "NT32", 9), c.NSLOT - 3

    def din(name, shape, dt=F32):
        return nc.dram_tensor(name, list(shape), dt, kind="ExternalInput").ap()

    def dout(name, shape, dt=F32):
        return nc.dram_tensor(name, list(shape), dt, kind="ExternalOutput").ap()

    xall = din("xall", [c.NROWS, D])
    memb = din("memb", [c.NMEM, D])
    kc_in = din("kcache", [c.NMEM, D])
    vc_in = din("vcache", [c.NMEM, D])
    sa_in = din("sta", [c.PA, DC])
    sb_in = din("stb", [c.PB, DC])
    cv_in = din("cvec", [c.NV, 128])
    id_in = din("ident", [128, 128])
    mk_in = din("mask", [128, 1])
    w32 = din("w32", [c.WSCR])
    yall = dout("yall", [c.NP + c.NS, D])
    nca = dout("nca", [2, c.PA, DC])
    ncb = dout("ncb", [2, c.PB, DC])
    nk_o = dout("nk", [c.NMEM, D])
    nv_o = dout("nv", [c.NMEM, D])
    wscr = nc.dram_tensor("wscr", [c.WSCR], BF16, kind="Internal").ap()

    AW = c.PA + TWB
    CW = c.PB + TWB
    lay = {}
    o = 0

    def region(name, nbytes):
        nonlocal o
        lay[name] = o
        o += (nbytes + 31) // 32 * 32

    region("X", ND * TWB * 4)
    region("U", ND * TWB * 2)
    o_cx_rel = (NCC * AW * 4 + 31) // 32 * 32
    big = max(NF * TWB * 2,
              o_cx_rel + max(NCC * CW * 4, NCC * TWB * 2, 2 * D * 4),
              2 * ND * TWB * 2 + 4 * TWB * 2,
              2 * NM * D * 4)
    region("BIG", big)
    region("MID", max(NXIN * D * 4, NCC * TWB * 4 + NCC * TWB * 2))
    region("T32", NT32 * TWB * 4)
    region("TBF", 4 * TWB * 2)
    region("KT", ND * c.NMEM * 2)
    region("VS", NM * D * 2)
    region("WS", c.NSLOT * 16 * 128 * 2)
    region("CVEC", c.NV * 4)
    region("IDENT", 128 * 4)
    region("ONES", 128 * 2)
    region("CARA", NCC * c.PA * 4)
    region("CARB", NCC * c.PB * 4)
    region("MASK", 4)
    region("STST", DC * 4)
    region("CVST", 128 * 4)
    region("STG", 32)
    ARENA = o
    cfg.arena_bytes = o
    cfg.lay = dict(lay)

    P = Prog()
    sem_names = ["c_pe", "c_act", "c_dve", "c_pool", "c_sp", "cst", "cvl", "sti", "kvi0", "kvi1",
                 "sg0", "sg1", "xin0", "xin1", "xin2", "xin3", "yo0", "yo1",
                 "so", "kvo0", "kvo1"] + ["ws%d" % i for i in range(c.NSLOT)] + ["wst%d" % i for i in range(c.NSLOT)] + ["wc%d" % i for i in range(c.NSLOT)]

    import contextlib
    with contextlib.ExitStack() as es:
        arena = es.enter_context(nc.sbuf_tensor("arena", [128, ARENA // 4], F32))
        ps = es.enter_context(nc.psum_tensor("ps", [128, 8 * 512], F32))
        sems = {n: es.enter_context(nc.semaphore(n)) for n in sem_names}
        block = es.enter_context(nc.Block())

        def f32v(off, n):
            return arena[:, off // 4: off // 4 + n]

        def bfv(off, n):
            return arena[:, off // 4: off // 4 + n // 2].bitcast(BF16)

        allbufs = []

        def mk(name, lo=None, n=None):
            b = Buf(name, lo, n)
            allbufs.append(b)
            return b

        oX, oU, BIG, MID = lay["X"], lay["U"], lay["BIG"], lay["MID"]
        X = f32v(oX, ND * TWB).rearrange("p (c t) -> p c t", c=ND)
        bX = [mk("x%d" % i, oX + i * TWB * 4, TWB * 4) for i in range(ND)]
        U = bfv(oU, ND * TWB).rearrange("p (c t) -> p c t", c=ND)
        bU = [mk("u%d" % i, oU + i * TWB * 2, TWB * 2) for i in range(ND)]
        ABUF = f32v(BIG, NCC * AW).rearrange("p (c t) -> p c t", c=NCC)
        bA = [mk("a%d" % i, BIG + i * AW * 4, AW * 4) for i in range(NCC)]
        o_cx = BIG + o_cx_rel
        CXB = f32v(o_cx, NCC * CW).rearrange("p (c t) -> p c t", c=NCC)
        bCX = [mk("cx%d" % i, o_cx + i * CW * 4, CW * 4) for i in range(NCC)]
        AACT = bfv(o_cx, NCC * TWB).rearrange("p (c t) -> p c t", c=NCC)
        bAACT = [mk("aact%d" % i, o_cx + i * TWB * 2, TWB * 2) for i in range(NCC)]
        YOUT = [f32v(o_cx + i * D * 4, D) for i in range(2)]
        bYOUT = [mk("yout%d" % i, o_cx + i * D * 4, D * 4) for i in range(2)]
        QT = bfv(BIG, ND * TWB).rearrange("p (c t) -> p c t", c=ND)
        bQT = [mk("qt%d" % i, BIG + i * TWB * 2, TWB * 2) for i in range(ND)]
        oOT = BIG + ND * TWB * 2
        OT = bfv(oOT, ND * TWB).rearrange("p (c t) -> p c t", c=ND)
        bOT = [mk("ot%d" % i, oOT + i * TWB * 2, TWB * 2) for i in range(ND)]
        oPT = BIG + 2 * ND * TWB * 2
        PT = bfv(oPT, 4 * TWB).rearrange("p (a m t) -> p a m t", a=2, m=2)
        bPT = [mk("pt%d" % i, oPT + i * 2 * TWB * 2, 2 * TWB * 2) for i in range(2)]
        HFF = bfv(BIG, NF * TWB).rearrange("p (c t) -> p c t", c=NF)
        bHFF = [mk("hff%d" % i, BIG + i * TWB * 2, TWB * 2) for i in range(NF)]
        KVSTG = [f32v(BIG + i * NM * D * 4, NM * D).rearrange("p (b f) -> p b f", b=NM) for i in range(2)]
        bKVSTG = [mk("kvstg%d" % i, BIG + i * NM * D * 4, NM * D * 4) for i in range(2)]
        XIN = [f32v(MID + i * D * 4, D) for i in range(NXIN)]
        bXIN = [mk("xin%d" % i, MID + i * D * 4, D * 4) for i in range(NXIN)]
        YB = f32v(MID, NCC * TWB).rearrange("p (c t) -> p c t", c=NCC)
        bY = [mk("y%d" % i, MID + i * TWB * 4, TWB * 4) for i in range(NCC)]
        assert ND * TWB * 2 <= NCC * TWB * 4
        MB = bfv(MID, ND * TWB).rearrange("p (c t) -> p c t", c=ND)
        bM = [mk("m%d" % i, MID + i * TWB * 2, TWB * 2) for i in range(ND)]
        oBBM = MID + NCC * TWB * 4
        BBM = bfv(oBBM, NCC * TWB).rearrange("p (c t) -> p c t", c=NCC)
        bBBM = [mk("bbm%d" % i, oBBM + i * TWB * 2, TWB * 2) for i in range(NCC)]
        T32 = [f32v(lay["T32"] + i * TWB * 4, TWB) for i in range(NT32)]
        bT32 = [mk("t32_%d" % i, lay["T32"] + i * TWB * 4, TWB * 4) for i in range(NT32)]
        TBF = [bfv(lay["TBF"] + i * TWB * 2, TWB) for i in range(4)]
        bTBF = [mk("tbf_%d" % i, lay["TBF"] + i * TWB * 2, TWB * 2) for i in range(4)]
        KT = bfv(lay["KT"], ND * c.NMEM).rearrange("p (c m) -> p c m", c=ND)
        VS = bfv(lay["VS"], NM * D).rearrange("p (b f) -> p b f", b=NM)
        bKT, bVS = mk("kt", lay["KT"], ND * c.NMEM * 2), mk("vs", lay["VS"], NM * D * 2)
        WS = [bfv(lay["WS"] + i * 4096, 2048) for i in range(c.NSLOT)]
        bWS = [mk("ws%d" % i, lay["WS"] + i * 4096, 4096) for i in range(c.NSLOT)]
        CVEC = f32v(lay["CVEC"], c.NV)
        IDENT = f32v(lay["IDENT"], 128)
        ONES = bfv(lay["ONES"], 128)
        CARA = f32v(lay["CARA"], NCC * c.PA).rearrange("p (c t) -> p c t", c=NCC)
        CARB = f32v(lay["CARB"], NCC * c.PB).rearrange("p (c t) -> p c t", c=NCC)
        MASK = f32v(lay["MASK"], 1)
        STST = f32v(lay["STST"], DC)
        CVST = f32v(lay["CVST"], 128)
        bCVEC, bIDENT, bONES, bMASK = mk("cvec"), mk("ident"), mk("ones"), mk("mask")
        bCARA, bCARB, bSTST, bCVST = mk("cara"), mk("carb"), mk("stst"), mk("cvst")
        bSTG = []
        bSCR = {key: mk("scr_%s_%d_%d" % key) for key in c.wtile}
        bPS = [mk("ps%d" % i) for i in range(8)]
        for b_ in bPS:
            b_.excl = True
        link_aliases(allbufs)

        state = dict(bank=0, stats=0, t32=0, tbf=0, ws=0, xin=0, yo=0, ring6=1)

        def next_bank():
            n = 6 if state["ring6"] else 5
            i = state["bank"] % n
            state["bank"] = (i + 1) % n
            return i

        def next_stats():
            i = state["stats"]
            state["stats"] = 1 - i
            return 6 + i

        def tmp32():
            i = state["t32"]
            state["t32"] = (i + 1) % NT32
            return T32[i], bT32[i]

        def tmpbf():
            i = state["tbf"]
            state["tbf"] = (i + 1) % 4
            return TBF[i], bTBF[i]

        def cv(col):
            return CVEC[:, col:col + 1]

        def pb(bi, W, c0=0):
            return ps[:, bi * 512 + c0: bi * 512 + c0 + W]

        wseq = []
        wm = dict(dry=True, i=0, issued=0, pending=None, fu=0, first={}, count={})

        def wview(sl):
            return WS[sl].rearrange("p (k n) -> p k n", n=128)

        def w_store_pending():
            pend = wm["pending"]
            if pend is None:
                return
            wm["pending"] = None
            sl, key, off, n = pend
            dst = wscr[off: off + 128 * n].rearrange("(p f) -> p f", p=128)
            src = WS[sl][:, 0:n]
            P.dma("sp", "wst%d" % sl, lambda e, dst=dst, src=src: e.dma_start(out=dst, in_=src),
                  reads=[bWS[sl]], writes=[bSCR[key]])

        def w_issue(i):
            key = wseq[i]
            name, jo, kg = key
            off, nk, k0 = c.wtile[key]
            sl = i % c.NSLOT
            n = nk * 128
            if wm["first"][key] == i:
                src32 = w32[off: off + 128 * n].rearrange("(p f) -> p f", p=128)
                dst = WS[sl][:, 0:n]
                P.dma("pool", "wc%d" % sl, lambda e, dst=dst, src32=src32: e.dma_start(out=dst, in_=src32),
                      reads=[], writes=[bWS[sl]])
                w_store_pending()
                if wm["count"][key] > 1:
                    wm["pending"] = (sl, key, off, n)
            else:
                src = wscr[off: off + 128 * n].rearrange("(p f) -> p f", p=128)
                dst = WS[sl][:, 0:n]
                P.dma("sp", "ws%d" % sl, lambda e, dst=dst, src=src: e.dma_start(out=dst, in_=src),
                      reads=[bSCR[key]], writes=[bWS[sl]])
                w_store_pending()

        def wload(name, jo, kg=0):
            key = (name, jo, kg)
            nk = c.wtile[key][1]
            if wm["dry"]:
                wseq.append(key)
                return wview(0), bWS[0], nk
            i = wm["i"]
            wm["i"] += 1
            assert wseq[i] == key, (i, wseq[i], key)
            while wm["issued"] < min(len(wseq), i + 1 + AHEAD):
                w_issue(wm["issued"])
                wm["issued"] += 1
            sl = i % c.NSLOT
            return wview(sl), bWS[sl], nk

        def mm_group(bi, pairs, col0, ncol, all_sig=False):
            n = len(pairs)
            out = pb(bi, ncol, col0)
            for i, (l, r, rb) in enumerate(pairs):
                P.op("pe", lambda e, out=out, l=l, r=r, i=i: e.matmul(out, lhsT=l, rhs=r, start=(i == 0), stop=(i == n - 1)),
                     reads=rb, writes=[bPS[bi]], sig=(all_sig or i == n - 1))

        def proj_ws(name, jo, act, bact, W, nkg=1):
            bi = next_bank()
            pairs = []
            for kg in range(nkg):
                wt, bw, nk = wload(name, jo, kg)
                k0 = c.wtile[(name, jo, kg)][2]
                for k in range(nk):
                    pairs.append((wt[:, k, :], act[:, k0 + k, 0:W], [bw, bact[k0 + k]]))
            mm_group(bi, pairs, 0, W)
            return bi

        def rms_stat_mm(st, W, k, sq, bsq):
            P.op("pe", lambda e: e.matmul(pb(st, W), lhsT=ONES, rhs=sq[:, 0:W], start=(k == 0), stop=(k == ND - 1)),
                 reads=[bsq, bONES], writes=[bPS[st]], sig=True)

        def rms_square(W, k):
            sq, bsq = tmpbf()
            P.op("act", lambda e: e.activation(out=sq[:, 0:W], in_=X[:, k, 0:W], func=AF.Square),
                 reads=[bX[k]], writes=[bsq])
            return sq, bsq

        def rms_finish(st, W, gcol, to_x=False):
            sd, bsd = tmp32()
            P.op("act", lambda e: e.activation(out=sd[:, 0:W], in_=pb(st, W), func=AF.Ln, scale=1.0 / D, bias=RMS_EPS),
                 reads=[bPS[st]], writes=[bsd])
            P.op("act", lambda e: e.activation(out=pb(st, W), in_=sd[:, 0:W], func=AF.Exp, scale=-0.5),
                 reads=[bsd], writes=[bPS[st]])
            for k in range(ND):
                dst = X[:, k, 0:W] if to_x else U[:, k, 0:W]
                P.op("dve", lambda e, k=k, dst=dst: e.scalar_tensor_tensor(
                        out=dst, in0=X[:, k, 0:W], scalar=cv(gcol + k), in1=pb(st, W), op0=ALU.mult, op1=ALU.mult),
                     reads=[bX[k], bPS[st], bCVEC], writes=[bX[k]] if to_x else [bU[k]])

        def rmsnorm(W, gcol, to_x=False):
            st = next_stats()
            for k in range(ND):
                sq, bsq = rms_square(W, k)
                rms_stat_mm(st, W, k, sq, bsq)
            rms_finish(st, W, gcol, to_x)

        def residual_phase(name, act, bact, W, nkg=1):
            st = next_stats()
            pend = []
            for jo in range(ND):
                b = proj_ws(name, jo, act, bact, W, nkg=nkg)
                P.op("dve", lambda e, b=b, jo=jo: e.tensor_tensor(out=X[:, jo, 0:W], in0=pb(b, W), in1=X[:, jo, 0:W], op=ALU.add),
                     reads=[bPS[b], bX[jo]], writes=[bX[jo]])
                sq, bsq = rms_square(W, jo)
                pend.append((jo, sq, bsq))
                if len(pend) > 2:
                    k, sq0, bsq0 = pend.pop(0)
                    rms_stat_mm(st, W, k, sq0, bsq0)
            for k, sq0, bsq0 in pend:
                rms_stat_mm(st, W, k, sq0, bsq0)
            return st

        def xin_dma(src, row, n, eng="pool"):
            s = state["xin"]
            state["xin"] = (s + 1) % NXIN
            dst = XIN[s][0:n, :]
            sr = src[row: row + n, :]
            P.dma(eng, "xin%d" % s, lambda e, dst=dst, sr=sr: e.dma_start(out=dst, in_=sr),
                  reads=[], writes=[bXIN[s]])
            return s

        def load_rows(src, row0, W, eng="pool"):
            blocks, slots, deferred = [], [], []
            r = 0
            while r < W:
                n = min(128, W - r)
                if len(blocks) < NXIN:
                    slots.append(xin_dma(src, row0 + r, n, eng))
                    blocks.append((r, n))
                else:
                    deferred.append((r, n, row0 + r))
                r += n
            return blocks, slots, deferred, src

        def transpose_in(blocks, slots, deferred=(), src=None):
            blocks, slots, deferred = list(blocks), list(slots), list(deferred)
            i = 0
            while i < len(blocks):
                (r, n), s = blocks[i], slots[i]
                i += 1
                for g in range(0, ND, 4):
                    bi = next_bank()
                    ng = min(4, ND - g)
                    for q in range(ng):
                        k = g + q
                        P.op("pe", lambda e, bi=bi, q=q, k=k, s=s, n=n: e.transpose(
                                pb(bi, n, q * 128), XIN[s][0:n, k * 128:(k + 1) * 128], IDENT[0:n, 0:n]),
                             reads=[bXIN[s], bIDENT], writes=[bPS[bi]], sig=(q == ng - 1))
                    srcp = pb(bi, ng * 128).rearrange("p (q t) -> p q t", q=ng)[:, :, 0:n]
                    dst = X[:, g:g + ng, r:r + n]
                    P.op("act", lambda e, srcp=srcp, dst=dst: e.activation(out=dst, in_=srcp, func=AF.Identity),
                         reads=[bPS[bi]], writes=[bX[g + q] for q in range(ng)])
                if deferred:
                    r2, n2, row2 = deferred.pop(0)
                    slots.append(xin_dma(src, row2, n2))
                    blocks.append((r2, n2))

        def state_in(src_ap, npre, car, bcar):
            P.dma("pool", "sti", lambda e: e.dma_start(out=STST[0:npre, :], in_=src_ap), reads=[], writes=[bSTST])
            bi = next_bank()
            for k in range(NCC):
                P.op("pe", lambda e, k=k, bi=bi: e.transpose(pb(bi, npre, k * npre), STST[0:npre, k * 128:(k + 1) * 128],
                                                           IDENT[0:npre, 0:npre]),
                     reads=[bSTST, bIDENT], writes=[bPS[bi]], sig=(k == NCC - 1))
            P.op("act", lambda e, bi=bi: e.activation(
                    out=car, in_=pb(bi, NCC * npre).rearrange("p (c t) -> p c t", c=NCC), func=AF.Identity),
                 reads=[bPS[bi]], writes=[bcar])

        def state_out(car, bcar, npre, dst_ap):
            for g in range(0, NCC, 4):
                bi = next_bank()
                ng = min(4, NCC - g)
                for q in range(ng):
                    k = g + q
                    P.op("pe", lambda e, bi=bi, q=q, k=k: e.transpose(
                            ps[0:npre, bi * 512 + q * 128: bi * 512 + (q + 1) * 128], car[:, k, :], IDENT),
                         reads=[bcar, bIDENT], writes=[bPS[bi]], sig=(q == ng - 1))
                P.op("act", lambda e, bi=bi, g=g, ng=ng: e.activation(
                        out=STST[0:npre, g * 128:(g + ng) * 128], in_=ps[0:npre, bi * 512: bi * 512 + ng * 128],
                        func=AF.Identity),
                     reads=[bPS[bi]], writes=[bSTST])
            P.dma("pool", "so", lambda e: e.dma_start(out=dst_ap, in_=STST[0:npre, :]), reads=[bSTST], writes=[])

        def stage(k):
            if getattr(cfg, "stop", None) is not None and k > cfg.stop:
                raise _Stop()

        def program():
          if True:
            P.dma("sp", "cst", lambda e: e.dma_start(out=IDENT, in_=id_in[:, :]), reads=[], writes=[bIDENT])
            P.dma("sp", "cst", lambda e: e.dma_start(out=MASK, in_=mk_in[:, :]), reads=[], writes=[bMASK])
            tot = P.dcnt["cst"]
            bIDENT.w = [(Ev("cst", tot), "dma")]
            bMASK.w = [(Ev("cst", tot), "dma")]
            P.op("pool", lambda e: e.memset(ONES, 1.0), reads=[], writes=[bONES])
            r0 = 0
            while r0 < c.NV:
                n = min(128, c.NV - r0)
                P.dma("sp", "cvl", lambda e, r0=r0, n=n: e.dma_start(out=CVST[0:n, :], in_=cv_in[r0:r0 + n, :]),
                      reads=[], writes=[bCVST])
                bi = next_bank()
                P.op("pe", lambda e, bi=bi, n=n: e.transpose(pb(bi, n), CVST[0:n, :], IDENT[0:n, 0:n]),
                     reads=[bCVST, bIDENT], writes=[bPS[bi]])
                P.op("act", lambda e, bi=bi, n=n, r0=r0: e.activation(out=CVEC[:, r0:r0 + n], in_=pb(bi, n), func=AF.Identity),
                     reads=[bPS[bi]], writes=[bCVEC])
                r0 += n

            scale_att = float(c.HD) ** -0.5

            def mixer(W, tile):
                halo = tile["halo"]
                sb = 4 + 10 * tile.get("ti", 0)
                if halo:
                    for j in range(NCC):
                        P.op("dve", lambda e, j=j: e.memset(ABUF[:, j, 0:c.PA], 0.0), reads=[], writes=[bA[j]])
                        P.op("dve", lambda e, j=j: e.memset(CXB[:, j, 0:c.PB], 0.0), reads=[], writes=[bCX[j]])
                else:
                    P.op("pool", lambda e: e.tensor_copy(out=ABUF[:, :, 0:c.PA], in_=CARA), reads=[bCARA], writes=bA)
                    P.op("pool", lambda e: e.tensor_copy(out=CXB[:, :, 0:c.PB], in_=CARB), reads=[bCARB], writes=bCX)
                stage(sb + 0.1)
                rmsnorm(W, c.V_NMIX)
                stage(sb + 0.2)
                s1, s2 = 6, 7
                state["ring6"] = 0
                for j in range(NCC):
                    bv = proj_ws("w_in", c.C_AV + j, U, bU, W)
                    bg = proj_ws("w_in", c.C_AG + j, U, bU, W)
                    sg, bsg = tmp32()
                    P.op("act", lambda e, bg=bg, sg=sg, j=j: e.activation(out=sg[:, 0:W], in_=pb(bg, W), func=AF.Sigmoid,
                                                                          bias=cv(c.V_BIN + c.C_AG + j)),
                         reads=[bPS[bg], bCVEC], writes=[bsg])
                    P.op("dve", lambda e, bv=bv, sg=sg, j=j: e.scalar_tensor_tensor(
                            out=ABUF[:, j, c.PA:c.PA + W], in0=pb(bv, W), scalar=cv(c.V_BIN + c.C_AV + j),
                            in1=sg[:, 0:W], op0=ALU.add, op1=ALU.mult),
                         reads=[bPS[bv], bsg, bCVEC], writes=[bA[j]])
                    if halo:
                        P.op("dve", lambda e, j=j: e.tensor_scalar(out=ABUF[:, j, c.PA:c.PA + halo], in0=ABUF[:, j, c.PA:c.PA + halo],
                                                                   scalar1=MASK, scalar2=None, op0=ALU.mult),
                             reads=[bA[j], bMASK], writes=[bA[j]])
                    bc_ = proj_ws("w_in", c.C_BC + j, U, bU, W)
                    bx_ = proj_ws("w_in", c.C_BX + j, U, bU, W)
                    bb_ = proj_ws("w_in", c.C_BB + j, U, bU, W)
                    xs, bxs = tmp32()
                    P.op("act", lambda e, bx_=bx_, xs=xs, j=j: e.activation(out=xs[:, 0:W], in_=pb(bx_, W), func=AF.Identity,
                                                                            bias=cv(c.V_BIN + c.C_BX + j)),
                         reads=[bPS[bx_], bCVEC], writes=[bxs])
                    cs, bcs = tmp32()
                    P.op("act", lambda e, bc_=bc_, cs=cs, j=j: e.activation(out=cs[:, 0:W], in_=pb(bc_, W), func=AF.Identity,
                                                                            bias=cv(c.V_BIN + c.C_BC + j)),
                         reads=[bPS[bc_], bCVEC], writes=[bcs])
                    P.op("pool", lambda e, cs=cs, xs=xs, j=j: e.tensor_tensor(out=CXB[:, j, c.PB:c.PB + W], in0=cs[:, 0:W],
                                                                              in1=xs[:, 0:W], op=ALU.mult),
                         reads=[bcs, bxs], writes=[bCX[j]])
                    if halo:
                        P.op("dve", lambda e, j=j: e.tensor_scalar(out=CXB[:, j, c.PB:c.PB + halo], in0=CXB[:, j, c.PB:c.PB + halo],
                                                                   scalar1=MASK, scalar2=None, op0=ALU.mult),
                             reads=[bCX[j], bMASK], writes=[bCX[j]])
                    cb, bcb = tmp32()
                    P.op("pool", lambda e, cb=cb, j=j: e.tensor_scalar(out=cb[:, 0:W], in0=CXB[:, j, 0:W],
                                                                       scalar1=cv(c.V_CBW + j), scalar2=None, op0=ALU.mult),
                         reads=[bCX[j], bCVEC], writes=[bcb])
                    for t in range(1, c.KB):
                        tt, btt = tmp32()
                        P.op("pool", lambda e, tt=tt, j=j, t=t: e.tensor_scalar(out=tt[:, 0:W], in0=CXB[:, j, t:t + W],
                                                                                scalar1=cv(c.V_CBW + t * NCC + j), scalar2=None, op0=ALU.mult),
                             reads=[bCX[j], bCVEC], writes=[btt])
                        P.op("pool", lambda e, tt=tt, cb=cb: e.tensor_tensor(out=cb[:, 0:W], in0=cb[:, 0:W], in1=tt[:, 0:W], op=ALU.add),
                             reads=[bcb, btt], writes=[bcb])
                    bs, bbs = tmp32()
                    P.op("act", lambda e, bb_=bb_, bs=bs, j=j: e.activation(out=bs[:, 0:W], in_=pb(bb_, W), func=AF.Identity,
                                                                            bias=cv(c.V_BIN + c.C_BB + j)),
                         reads=[bPS[bb_], bCVEC], writes=[bbs])
                    P.op("pool", lambda e, bs=bs, cb=cb, j=j: e.tensor_tensor(out=BBM[:, j, 0:W], in0=bs[:, 0:W], in1=cb[:, 0:W], op=ALU.mult),
                         reads=[bbs, bcb], writes=[bBBM[j]])
                    acc = pb(5, W)
                    P.op("dve", lambda e, j=j, acc=acc: e.tensor_scalar(out=acc, in0=ABUF[:, j, 0:W], scalar1=cv(c.V_CAW + j),
                                                                        scalar2=cv(c.V_CAB + j), op0=ALU.mult, op1=ALU.add),
                         reads=[bA[j], bCVEC], writes=[bPS[5]])
                    for t in range(1, c.KA):
                        P.op("dve", lambda e, j=j, t=t, acc=acc: e.scalar_tensor_tensor(
                                out=acc, in0=ABUF[:, j, t:t + W], scalar=cv(c.V_CAW + t * NCC + j), in1=acc,
                                op0=ALU.mult, op1=ALU.add),
                             reads=[bA[j], bPS[5], bCVEC], writes=[bPS[5]])
                    P.op("act", lambda e, j=j, acc=acc: e.activation(out=YB[:, j, 0:W], in_=acc, func=AF.Identity),
                         reads=[bPS[5]], writes=[bY[j]])
                    yb, byb = tmpbf()
                    P.op("act", lambda e, yb=yb, acc=acc: e.activation(out=yb[:, 0:W], in_=acc, func=AF.Identity),
                         reads=[bPS[5]], writes=[byb])
                    sq, bsq = tmpbf()
                    P.op("act", lambda e, sq=sq, acc=acc: e.activation(out=sq[:, 0:W], in_=acc, func=AF.Square),
                         reads=[bPS[5]], writes=[bsq])
                    if j >= 1:
                        pend()
                    def pend(yb=yb, sq=sq, byb=byb, bsq=bsq, j=j):
                        P.op("pe", lambda e: e.matmul(pb(s1, W), lhsT=ONES, rhs=yb[:, 0:W], start=(j == 0), stop=(j == NCC - 1)),
                             reads=[byb, bONES], writes=[bPS[s1]], sig=True)
                        P.op("pe", lambda e: e.matmul(pb(s2, W), lhsT=ONES, rhs=sq[:, 0:W], start=(j == 0), stop=(j == NCC - 1)),
                             reads=[bsq, bONES], writes=[bPS[s2]], sig=True)
                pend()
                state["ring6"] = 1
                P.op("pool", lambda e: e.tensor_copy(out=CARA, in_=ABUF[:, :, W:W + c.PA]), reads=bA, writes=[bCARA])
                P.op("pool", lambda e: e.tensor_copy(out=CARB, in_=CXB[:, :, W:W + c.PB]), reads=bCX, writes=[bCARB])
                stage(sb + 0.6)
                mean, bmean = tmp32()
                msq, bmsq = tmp32()
                var, bvar = tmp32()
                P.op("dve", lambda e: e.tensor_scalar(out=mean[:, 0:W], in0=pb(s1, W), scalar1=1.0 / DC, scalar2=None, op0=ALU.mult),
                     reads=[bPS[s1]], writes=[bmean])
                stage(sb + 0.61)
                P.op("act", lambda e: e.activation(out=msq[:, 0:W], in_=pb(s1, W), func=AF.Square, scale=1.0 / DC),
                     reads=[bPS[s1]], writes=[bmsq])
                stage(sb + 0.62)
                P.op("dve", lambda e: e.scalar_tensor_tensor(out=var[:, 0:W], in0=pb(s2, W), scalar=1.0 / DC, in1=msq[:, 0:W],
                                                             op0=ALU.mult, op1=ALU.subtract),
                     reads=[bPS[s2], bmsq], writes=[bvar])
                stage(sb + 0.63)
                P.op("act", lambda e: e.activation(out=var[:, 0:W], in_=var[:, 0:W], func=AF.Ln, bias=LN_EPS),
                     reads=[bvar], writes=[bvar])
                stage(sb + 0.64)
                P.op("act", lambda e: e.activation(out=pb(s2, W), in_=var[:, 0:W], func=AF.Exp, scale=-0.5),
                     reads=[bvar], writes=[bPS[s2]])
                stage(sb + 0.65)
                for j in range(NCC):
                    t_, bt_ = (msq, bmsq) if j % 2 == 0 else (var, bvar)
                    P.op("pool", lambda e, j=j, t_=t_: e.tensor_tensor(out=t_[:, 0:W], in0=YB[:, j, 0:W], in1=mean[:, 0:W],
                                                                       op=ALU.subtract),
                         reads=[bY[j], bmean], writes=[bt_])
                    stage(sb + 0.66)
                    P.op("dve", lambda e, t_=t_: e.tensor_tensor(out=t_[:, 0:W], in0=pb(s2, W), in1=t_[:, 0:W], op=ALU.mult),
                         reads=[bPS[s2], bt_], writes=[bt_])
                    stage(sb + 0.67)
                    P.op("act", lambda e, j=j, t_=t_: e.activation(out=AACT[:, j, 0:W], in_=t_[:, 0:W], func=AF.Silu,
                                                                   scale=cv(c.V_LNG + j), bias=cv(c.V_LNB + j)),
                         reads=[bt_, bCVEC], writes=[bAACT[j]])
                stage(sb + 0.7)
                def gates(jo):
                    b2 = proj_ws("w_in", c.C_GA + jo, U, bU, W)
                    b4 = proj_ws("w_in", c.C_GB + jo, U, bU, W)
                    b3 = proj_ws("w_b_out", jo, BBM, bBBM, W)
                    sA, bsA = tmp32()
                    sB, bsB = tmp32()
                    P.op("act", lambda e: e.activation(out=sA[:, 0:W], in_=pb(b2, W), func=AF.Sigmoid,
                                                       bias=cv(c.V_BIN + c.C_GA + jo)),
                         reads=[bPS[b2], bCVEC], writes=[bsA])
                    P.op("act", lambda e: e.activation(out=sB[:, 0:W], in_=pb(b4, W), func=AF.Sigmoid,
                                                       bias=cv(c.V_BIN + c.C_GB + jo)),
                         reads=[bPS[b4], bCVEC], writes=[bsB])
                    P.op("dve", lambda e: e.tensor_tensor(out=sB[:, 0:W], in0=pb(b3, W), in1=sB[:, 0:W], op=ALU.mult),
                         reads=[bPS[b3], bsB], writes=[bsB])
                    return sA, bsA, sB, bsB

                cur = gates(0)
                for jo in range(ND):
                    nxt = gates(jo + 1) if jo + 1 < ND else None
                    b1 = proj_ws("w_a_out", jo, AACT, bAACT, W)
                    sA, bsA, sB, bsB = cur
                    P.op("dve", lambda e, b1=b1, sA=sA, jo=jo: e.scalar_tensor_tensor(
                            out=sA[:, 0:W], in0=pb(b1, W), scalar=cv(c.V_BAO + jo), in1=sA[:, 0:W], op0=ALU.add, op1=ALU.mult),
                         reads=[bPS[b1], bsA, bCVEC], writes=[bsA])
                    P.op("pool", lambda e, sA=sA, sB=sB, jo=jo: e.tensor_tensor(out=MB[:, jo, 0:W], in0=sA[:, 0:W], in1=sB[:, 0:W],
                                                                                op=ALU.add),
                         reads=[bsA, bsB], writes=[bM[jo]])
                    cur = nxt
                stage(sb + 0.8)
                return residual_phase("w_mix_out", MB, bM, W)

            def attention(W, st_in):
                rms_finish(st_in, W, c.V_NX)
                for jo in range(ND):
                    b = proj_ws("w_q", jo, U, bU, W)
                    P.op("act", lambda e, b=b, jo=jo: e.activation(out=QT[:, jo, 0:W], in_=pb(b, W), func=AF.Identity),
                         reads=[bPS[b]], writes=[bQT[jo]])
                def scores(h):
                    par = h % 2
                    for mb in range(NM):
                        bi = next_bank()
                        pairs = [(KT[:, h * c.HC + dc, mb * 128:(mb + 1) * 128], QT[:, h * c.HC + dc, 0:W], [bKT, bQT[h * c.HC + dc]])
                                 for dc in range(c.HC)]
                        mm_group(bi, pairs, 0, W)
                        P.op("act", lambda e, bi=bi, par=par, mb=mb: e.activation(out=PT[:, par, mb, 0:W], in_=pb(bi, W),
                                                                                  func=AF.Exp, scale=scale_att),
                             reads=[bPS[bi]], writes=[bPT[par]])

                scores(0)
                for h in range(c.NH):
                    par = h % 2
                    if h + 1 < c.NH:
                        scores(h + 1)
                    st = next_stats()
                    mm_group(st, [(ONES, PT[:, par, mb, 0:W], [bPT[par], bONES]) for mb in range(NM)], 0, W)
                    rd, brd = tmp32()
                    P.op("dve", lambda e, st=st, rd=rd: e.reciprocal(out=rd[:, 0:W], in_=pb(st, W)), reads=[bPS[st]], writes=[brd])
                    for dc in range(c.HC):
                        bi = next_bank()
                        col = h * c.HD + dc * 128
                        pairs = [(VS[:, mb, col:col + 128], PT[:, par, mb, 0:W], [bVS, bPT[par]]) for mb in range(NM)]
                        mm_group(bi, pairs, 0, W)
                        P.op("dve", lambda e, bi=bi, rd=rd, k=h * c.HC + dc: e.tensor_tensor(
                                out=OT[:, k, 0:W], in0=pb(bi, W), in1=rd[:, 0:W], op=ALU.mult),
                             reads=[bPS[bi], brd], writes=[bOT[h * c.HC + dc]])
                return residual_phase("w_o", OT, bOT, W)

            def ffn(W, st_in):
                rms_finish(st_in, W, c.V_NFFN)
                for f in range(NF):
                    bg = proj_ws("w_gate", f, U, bU, W)
                    bu = proj_ws("w_up", f, U, bU, W)
                    sg, bsg = tmp32()
                    P.op("act", lambda e, bg=bg, sg=sg: e.activation(out=sg[:, 0:W], in_=pb(bg, W), func=AF.Silu),
                         reads=[bPS[bg]], writes=[bsg])
                    P.op("dve", lambda e, bu=bu, sg=sg, f=f: e.tensor_tensor(out=HFF[:, f, 0:W], in0=pb(bu, W), in1=sg[:, 0:W], op=ALU.mult),
                         reads=[bPS[bu], bsg], writes=[bHFF[f]])
                return residual_phase("w_down", HFF, bHFF, W, nkg=c.nkg["w_down"])

            def final_out(W, tile, st_in):
                rms_finish(st_in, W, c.V_NFIN, to_x=True)
                for (c0, n_out, orow) in tile["outs"]:
                    r = 0
                    while r < n_out:
                        n = min(128, n_out - r)
                        s = state["yo"]
                        state["yo"] = 1 - s
                        for g in range(0, ND, 4):
                            bi = next_bank()
                            ng = min(4, ND - g)
                            for q in range(ng):
                                k = g + q
                                P.op("pe", lambda e, bi=bi, q=q, k=k, n=n, cc=c0 + r: e.transpose(
                                        ps[0:n, bi * 512 + q * 128: bi * 512 + (q + 1) * 128], X[:, k, cc:cc + n], IDENT),
                                     reads=[bX[k], bIDENT], writes=[bPS[bi]], sig=(q == ng - 1))
                            P.op("act", lambda e, bi=bi, g=g, ng=ng, n=n, s=s: e.activation(
                                    out=YOUT[s][0:n, g * 128:(g + ng) * 128], in_=ps[0:n, bi * 512: bi * 512 + ng * 128], func=AF.Identity),
                                 reads=[bPS[bi]], writes=[bYOUT[s]])
                        dst = yall[orow + r: orow + r + n, :]
                        P.dma("pool", "yo%d" % s, lambda e, dst=dst, s=s, n=n: e.dma_start(out=dst, in_=YOUT[s][0:n, :]),
                              reads=[bYOUT[s]], writes=[])
                        r += n

            def kv_from_cache():
                for i, src in enumerate((kc_in, vc_in)):
                    P.dma("pool", "kvi%d" % i, lambda e, i=i, src=src: e.dma_start(
                            out=KVSTG[i], in_=src.rearrange("(b p) f -> p b f", p=128)),
                          reads=[], writes=[bKVSTG[i]])
                per = 512 // (NM * 128)
                for k in range(0, ND, per):
                    bi = next_bank()
                    nq = min(per, ND - k)
                    for q in range(nq):
                        for mb in range(NM):
                            P.op("pe", lambda e, bi=bi, q=q, mb=mb, k=k: e.transpose(
                                    pb(bi, 128, (q * NM + mb) * 128), KVSTG[0][:, mb, (k + q) * 128:(k + q + 1) * 128], IDENT),
                                 reads=[bKVSTG[0], bIDENT], writes=[bPS[bi]], sig=(q == nq - 1 and mb == NM - 1))
                    P.op("act", lambda e, bi=bi, k=k, nq=nq: e.activation(
                            out=KT[:, k:k + nq, :], in_=pb(bi, nq * NM * 128).rearrange("p (q m) -> p q m", q=nq), func=AF.Identity),
                         reads=[bPS[bi]], writes=[bKT])
                for mb in range(NM):
                    P.op("act", lambda e, mb=mb: e.activation(out=VS[:, mb, :], in_=KVSTG[1][:, mb, :], func=AF.Identity),
                         reads=[bKVSTG[1]], writes=[bVS])

            def kv_from_mem():
                W = c.NMEM
                transpose_in(*load_rows(memb, 0, W))
                rmsnorm(W, c.V_NMEM)
                stage(9.1)
                for wi, (wname, dst_o) in enumerate((("w_k", nk_o), ("w_v", nv_o))):
                    stage(9.2 + 0.4 * wi)
                    for jo in range(ND):
                        wt, bw, nk = wload(wname, jo, 0)
                        if wi == 0:
                            bi = next_bank()
                            mm_group(bi, [(wt[:, k, :], U[:, k, 0:W], [bw, bU[k]]) for k in range(ND)], 0, W)
                            P.op("act", lambda e, bi=bi, jo=jo: e.activation(out=KT[:, jo, :], in_=pb(bi, W), func=AF.Identity),
                                 reads=[bPS[bi]], writes=[bKT])
                        bi = next_bank()
                        for mb in range(NM):
                            mm_group(bi, [(U[:, k, mb * 128:(mb + 1) * 128], wt[:, k, :], [bw, bU[k]]) for k in range(ND)], mb * 128, 128)
                        src = pb(bi, NM * 128).rearrange("p (b n) -> p b n", b=NM)
                        P.op("act", lambda e, src=src, wi=wi, jo=jo: e.activation(out=KVSTG[wi][:, :, jo * 128:(jo + 1) * 128], in_=src,
                                                                                  func=AF.Identity),
                             reads=[bPS[bi]], writes=[bKVSTG[wi]])
                        if wi == 1:
                            P.op("act", lambda e, src=src, jo=jo: e.activation(out=VS[:, :, jo * 128:(jo + 1) * 128], in_=src,
                                                                               func=AF.Identity),
                                 reads=[bPS[bi]], writes=[bVS])
                    stage(9.3 + 0.4 * wi)
                    P.dma("pool", "kvo%d" % wi, lambda e, wi=wi, dst_o=dst_o: e.dma_start(
                            out=dst_o.rearrange("(b p) f -> p b f", p=128), in_=KVSTG[wi]),
                          reads=[bKVSTG[wi]], writes=[])

            stage(2)
            tiles = c.tiles
            kv_from_mem()
            pre = load_rows(xall, tiles[0]["row0"], tiles[0]["W"])
            for ti, tile in enumerate(tiles):
                W = tile["W"]
                tile["ti"] = ti
                if tile["kind"] == "sample":
                    state_out(CARA, bCARA, c.PA, nca[0])
                    state_out(CARB, bCARB, c.PB, ncb[0])
                    state_in(sa_in[:, :], c.PA, CARA, bCARA)
                    state_in(sb_in[:, :], c.PB, CARB, bCARB)
                stage(3 + 10 * ti)
                transpose_in(*pre)
                stage(4 + 10 * ti)
                st_ = mixer(W, tile)
                stage(5 + 10 * ti)
                if ti + 1 < len(tiles):
                    nt_ = tiles[ti + 1]
                    pre = load_rows(xall, nt_["row0"], nt_["W"])
                stage(6 + 10 * ti)
                st_ = attention(W, st_)
                if ti + 1 < len(tiles) and tiles[ti + 1]["kind"] == "sample":
                    kv_from_cache()
                stage(7 + 10 * ti)
                st_ = ffn(W, st_)
                stage(8 + 10 * ti)
                final_out(W, tile, st_)
                stage(9 + 10 * ti)
            state_out(CARA, bCARA, c.PA, nca[1])
            state_out(CARB, bCARB, c.PB, ncb[1])
            w_store_pending()

        def reset_tracking():
            P.__init__()
            for b_ in allbufs:
                b_.w = []
                b_.r = {}
            for k_ in state:
                state[k_] = 0
            state["ring6"] = 1

        saved_stop = getattr(cfg, "stop", None)
        cfg.stop = None
        program()
        cfg.stop = saved_stop
        for i_, key_ in enumerate(wseq):
            wm["first"].setdefault(key_, i_)
            wm["count"][key_] = wm["count"].get(key_, 0) + 1
        wm["dry"] = False
        reset_tracking()
        try:
            program()
        except _Stop:
            pass
        P.wait_all("pool", ["yo0", "yo1", "so", "kvo0", "kvo1"])
        P.wait_all("sp", ["cst", "cvl", "sg0", "sg1"] + ["ws%d" % i for i in range(c.NSLOT)] + ["wst%d" % i for i in range(c.NSLOT)])
        P.wait_all("pool", ["sti", "kvi0", "kvi1", "xin0", "xin1", "xin2"] + ["wc%d" % i for i in range(c.NSLOT)])

        def replay(eng_name):
            def run(e):
                for item in P.q[eng_name]:
                    if item[0] == "wait":
                        e.wait_ge(sems[item[1]], item[2])
                    else:
                        ins = item[1](e)
                        if item[2] is not None:
                            ins.then_inc(sems[item[2]], item[3])
            return run

        block.sync(replay("sp"))
        block.tensor(replay("pe"))
        block.scalar(replay("act"))
        block.vector(replay("dve"))
        block.gpsimd(replay("pool"))
        cfg.stats = {e: len(P.q[e]) for e in P.ENGS}
    return nc


def make_in_maps(cfg, inputs):
    c = cfg
    f = lambda a: np.ascontiguousarray(np.asarray(a, dtype=np.float32))
    xp = f(inputs["x_prompt"])
    xs = f(inputs["x_sample"])
    B, S, D = xp.shape
    segs = S // c.NP
    vec_names = ["norm_mix_g", "b_in", "conv_a_w", "conv_a_b", "ln_a_g", "ln_a_b", "b_a_out", "conv_b_w",
                 "norm_x_g", "norm_mem_g", "norm_ffn_g", "norm_final_g"]
    cvec = np.concatenate([f(inputs[n]).reshape(-1, 128) for n in vec_names], axis=0)
    assert cvec.shape[0] == c.NV, (cvec.shape, c.NV)
    ident = np.eye(128, dtype=np.float32)
    w32 = np.empty((c.WSCR,), np.float32)
    for name, K, N in c.weights:
        Wm = f(inputs[name])[0]
        nkc, njo = K // 128, N // 128
        W4 = Wm.reshape(nkc, 128, njo, 128)
        for (nm, jo, kg), (off, nk, k0) in c.wtile.items():
            if nm != name:
                continue
            t_ = W4[k0:k0 + nk, :, jo, :].transpose(1, 0, 2)
            w32[off: off + 128 * nk * 128] = t_.reshape(-1)
    in_maps = []
    for core in range(NCORES):
        b, s = core // segs, core % segs
        t0 = s * c.NP
        xall = np.empty((c.NROWS, D), np.float32)
        if s == 0:
            xall[0:c.HALO] = 0.0
        else:
            xall[0:c.HALO] = xp[b, t0 - c.HALO:t0]
        xall[c.HALO:c.HALO + c.NP] = xp[b, t0:t0 + c.NP]
        xall[c.HALO + c.NP:] = xs[core]
        m = dict(
            xall=xall,
            memb=f(inputs["mem_prompt"])[b],
            kcache=f(inputs["cache_mem_k"])[0, core].reshape(c.NMEM, D),
            vcache=f(inputs["cache_mem_v"])[0, core].reshape(c.NMEM, D),
            sta=f(inputs["state_conv_a"])[0, core],
            stb=f(inputs["state_conv_b"])[0, core],
            cvec=cvec,
            ident=ident,
            mask=np.full((128, 1), 0.0 if s == 0 else 1.0, np.float32),
        )
        m["w32"] = w32
        in_maps.append(m)
    return in_maps, (B, S, D, segs)


def assemble(cfg, res, meta):
    c = cfg
    B, S, D, segs = meta
    y_prompt = np.empty((B, S, D), np.float32)
    y_sample = np.empty((NCORES, c.NS, D), np.float32)
    nca_p = np.empty((1, B, c.PA, c.DC), np.float32)
    ncb_p = np.empty((1, B, c.PB, c.DC), np.float32)
    nk_p = np.empty((1, B, c.NMEM, c.NH, c.HD), np.float32)
    nv_p = np.empty((1, B, c.NMEM, c.NH, c.HD), np.float32)
    nca_s = np.empty((1, NCORES, c.PA, c.DC), np.float32)
    ncb_s = np.empty((1, NCORES, c.PB, c.DC), np.float32)
    for core in range(NCORES):
        r = res[core]
        b, s = core // segs, core % segs
        y_prompt[b, s * c.NP:(s + 1) * c.NP] = r["yall"][0:c.NP]
        y_sample[core] = r["yall"][c.NP:]
        nca_s[0, core] = r["nca"][1]
        ncb_s[0, core] = r["ncb"][1]
        if s == segs - 1:
            nca_p[0, b] = r["nca"][0]
            ncb_p[0, b] = r["ncb"][0]
        if s == 0:
            nk_p[0, b] = r["nk"].reshape(c.NMEM, c.NH, c.HD)
            nv_p[0, b] = r["nv"].reshape(c.NMEM, c.NH, c.HD)
    return (y_prompt, y_sample, nca_p, ncb_p, nk_p, nv_p, nca_s, ncb_s)


def run(cfg, inputs):
    nc = build_program(cfg)
    in_maps, meta = make_in_maps(cfg, inputs)
    res = run_bass_kernel_spmd(nc, in_maps, core_ids=list(range(NCORES)))
    return assemble(cfg, res.results, meta)


def kernel(**inputs):
    return run(Cfg(), inputs)
```

```python
import numpy as np
import concourse.bass as bass
import concourse.mybir as mybir
from concourse.bass_utils import run_bass_kernel_spmd

F32 = mybir.dt.float32
BF16 = mybir.dt.bfloat16
AF = mybir.ActivationFunctionType
ALU = mybir.AluOpType

RMS_EPS = 1e-6
LN_EPS = 1e-5
NCORES = 8


class Cfg:
    def __init__(self, D=2048, DFF=5632, NMEM=256, NH=4, NP=4096, TW=464, KA=31, KB=3,
                 NS=32, HALO=32, NSLOT=14):
        self.D, self.DFF, self.NMEM, self.NH, self.NP, self.TW = D, DFF, NMEM, NH, NP, TW
        self.KA, self.KB, self.NS, self.HALO, self.NSLOT = KA, KB, NS, HALO, NSLOT
        self.DC = D // 2
        self.ND, self.NCC, self.NF, self.NM = D // 128, self.DC // 128, DFF // 128, NMEM // 128
        self.HD = D // NH
        self.HC = self.HD // 128
        self.DIN = 5 * self.DC + 2 * D
        self.NIN = self.DIN // 128
        n = self.NCC
        self.C_AV, self.C_AG, self.C_BB, self.C_BC, self.C_BX = 0, n, 2 * n, 3 * n, 4 * n
        self.C_GA, self.C_GB = 5 * n, 5 * n + self.ND
        self.PA, self.PB = KA - 1, KB - 1
        o = 0
        def take(k):
            nonlocal o
            r = o
            o += k
            return r
        self.V_NMIX = take(self.ND)
        self.V_BIN = take(self.NIN)
        self.V_CAW = take(KA * n)
        self.V_CAB = take(n)
        self.V_LNG = take(n)
        self.V_LNB = take(n)
        self.V_BAO = take(self.ND)
        self.V_CBW = take(KB * n)
        self.V_NX = take(self.ND)
        self.V_NMEM = take(self.ND)
        self.V_NFFN = take(self.ND)
        self.V_NFIN = take(self.ND)
        self.NV = o
        self.weights = [
            ("w_in", D, self.DIN), ("w_a_out", self.DC, D), ("w_b_out", self.DC, D),
            ("w_mix_out", D, D), ("w_q", D, D), ("w_k", D, D), ("w_v", D, D), ("w_o", D, D),
            ("w_gate", D, DFF), ("w_up", D, DFF), ("w_down", DFF, D),
        ]
        self.wtile = {}
        self.units = []
        off = 0
        for name, K, N in self.weights:
            nkc, njo = K // 128, N // 128
            kgs = [(k0, min(16, nkc - k0)) for k0 in range(0, nkc, 16)]
            for kg, (k0, nk) in enumerate(kgs):
                for jo0 in range(0, njo, 4):
                    nj = min(4, njo - jo0)
                    self.units.append((name, k0, nk, jo0, nj, off))
                    for j in range(nj):
                        self.wtile[(name, jo0 + j, kg)] = (off, nk, k0)
                        off += 128 * nk * 128
        self.WSCR = off
        self.nkg = {name: (K // 128 + 15) // 16 for name, K, N in self.weights}
        tot = HALO + NP
        nt = -(-tot // TW)
        base = -(-tot // nt)
        base = -(-base // 8) * 8
        widths = []
        rem = tot
        while rem > 0:
            w = min(base, rem)
            widths.append(w)
            rem -= w
        self.tiles = []
        r = 0
        for i, w in enumerate(widths):
            h = HALO if i == 0 else 0
            self.tiles.append(dict(W=w, row0=r, kind="prompt", outs=[(h, w - h, r + h - HALO)], halo=h))
            r += w
        self.tiles.append(dict(W=NS, row0=HALO + NP, kind="sample", outs=[(0, NS, NP)], halo=0))
        self.NROWS = HALO + NP + NS


class _Stop(Exception):
    pass


class Ev:
    __slots__ = ("sem", "val")

    def __init__(self, sem, val):
        self.sem, self.val = sem, val


class Buf:
    __slots__ = ("name", "w", "r", "lo", "hi", "al", "excl")

    def __init__(self, name, lo=None, n=None):
        self.name = name
        self.excl = False
        self.w = []
        self.r = {}
        self.lo = lo
        self.hi = None if lo is None else lo + n
        self.al = ()


def link_aliases(bufs):
    rb = [b for b in bufs if b.lo is not None]
    for b in rb:
        b.al = tuple(o for o in rb if o is not b and o.lo < b.hi and b.lo < o.hi)


class Prog:
    ENGS = ("pe", "act", "dve", "pool", "sp")

    def __init__(self):
        self.q = {e: [] for e in self.ENGS}
        self.cnt = {e: 0 for e in self.ENGS}
        self.known = {e: {} for e in self.ENGS}
        self.dcnt = {}

    def _waits(self, eng, reads, writes):
        need = {}
        def add(ev, src, raw):
            if src == eng and (eng == "pe" or not raw):
                return
            if need.get(ev.sem, 0) < ev.val:
                need[ev.sem] = ev.val
        for b in reads:
            for ev, src in b.w:
                add(ev, src, True)
            if b.excl:
                for ev, src in b.r.values():
                    add(ev, src, False)
        for b0 in writes:
            for b in (b0,) + tuple(b0.al):
                for ev, src in b.w:
                    add(ev, src, False)
                for ev, src in b.r.values():
                    add(ev, src, False)
        kn = self.known[eng]
        for sem, val in need.items():
            if kn.get(sem, 0) < val:
                kn[sem] = val
                self.q[eng].append(("wait", sem, val))

    def _commit(self, ev, src, reads, writes):
        for b0 in writes:
            for b in (b0,) + tuple(b0.al):
                b.w = [(ev, src)]
                b.r = {}
        for b in reads:
            old = b.r.get(ev.sem)
            if old is None or old[0].val < ev.val:
                b.r[ev.sem] = (ev, src)

    def op(self, eng, fn, reads=(), writes=(), sig=True):
        self._waits(eng, reads, writes)
        if sig:
            self.cnt[eng] += 1
            ev = Ev("c_" + eng, self.cnt[eng])
            self.q[eng].append(("op", fn, "c_" + eng, 1))
        else:
            ev = Ev("c_" + eng, self.cnt[eng] + 1)
            self.q[eng].append(("op", fn, None, 0))
        self._commit(ev, eng, reads, writes)
        return ev

    def dma(self, eng, sem, fn, reads=(), writes=()):
        self._waits(eng, reads, writes)
        self.dcnt[sem] = self.dcnt.get(sem, 0) + 16
        ev = Ev(sem, self.dcnt[sem])
        self.q[eng].append(("op", fn, sem, 16))
        self._commit(ev, "dma", reads, writes)
        return ev

    def wait_all(self, eng, sems):
        for s in sems:
            v = self.dcnt.get(s, 0)
            if v and self.known[eng].get(s, 0) < v:
                self.known[eng][s] = v
                self.q[eng].append(("wait", s, v))


def build_program(cfg):
    c = cfg
    nc = bass.Bass("TRN2", target_bir_lowering=False)
    D, DC, ND, NCC, NF, NM = c.D, c.DC, c.ND, c.NCC, c.NF, c.NM
    TWB = -(-max([t["W"] for t in c.tiles] + [c.NMEM]) // 16) * 16
    NXIN, NT32, AHEAD = 3, getattr(cfg, "NT32", 9), c.NSLOT - 3

    def din(name, shape, dt=F32):
        return nc.dram_tensor(name, list(shape), dt, kind="ExternalInput").ap()

    def dout(name, shape, dt=F32):
        return nc.dram_tensor(name, list(shape), dt, kind="ExternalOutput").ap()

    xall = din("xall", [c.NROWS, D])
    memb = din("memb", [c.NMEM, D])
    kc_in = din("kcache", [c.NMEM, D])
    vc_in = din("vcache", [c.NMEM, D])
    sa_in = din("sta", [c.PA, DC])
    sb_in = din("stb", [c.PB, DC])
    cv_in = din("cvec", [c.NV, 128])
    id_in = din("ident", [128, 128])
    mk_in = din("mask", [128, 1])
    w32 = din("w32", [c.WSCR])
    yall = dout("yall", [c.NP + c.NS, D])
    nca = dout("nca", [2, c.PA, DC])
    ncb = dout("ncb", [2, c.PB, DC])
    nk_o = dout("nk", [c.NMEM, D])
    nv_o = dout("nv", [c.NMEM, D])
    wscr = nc.dram_tensor("wscr", [c.WSCR], BF16, kind="Internal").ap()

    AW = c.PA + TWB
    CW = c.PB + TWB
    lay = {}
    o = 0

    def region(name, nbytes):
        nonlocal o
        lay[name] = o
        o += (nbytes + 31) // 32 * 32

    region("X", ND * TWB * 4)
    region("U", ND * TWB * 2)
    o_cx_rel = (NCC * AW * 4 + 31) // 32 * 32
    big = max(NF * TWB * 2,
              o_cx_rel + max(NCC * CW * 4, NCC * TWB * 2, 2 * D * 4),
              2 * ND * TWB * 2 + 4 * TWB * 2,
              2 * NM * D * 4)
    region("BIG", big)
    region("MID", max(NXIN * D * 4, NCC * TWB * 4 + NCC * TWB * 2))
    region("T32", NT32 * TWB * 4)
    region("TBF", 4 * TWB * 2)
    region("KT", ND * c.NMEM * 2)
    region("VS", NM * D * 2)
    region("WS", c.NSLOT * 16 * 128 * 2)
    region("CVEC", c.NV * 4)
    region("IDENT", 128 * 4)
    region("ONES", 128 * 2)
    region("CARA", NCC * c.PA * 4)
    region("CARB", NCC * c.PB * 4)
    region("MASK", 4)
    region("STST", DC * 4)
    region("CVST", 128 * 4)
    region("STG", 32)
    ARENA = o
    cfg.arena_bytes = o
    cfg.lay = dict(lay)

    P = Prog()
    sem_names = ["c_pe", "c_act", "c_dve", "c_pool", "c_sp", "cst", "cvl", "sti", "kvi0", "kvi1",
                 "sg0", "sg1", "xin0", "xin1", "xin2", "xin3", "yo0", "yo1",
                 "so", "kvo0", "kvo1"] + ["ws%d" % i for i in range(c.NSLOT)] + ["wst%d" % i for i in range(c.NSLOT)] + ["wc%d" % i for i in range(c.NSLOT)]

    import contextlib
    with contextlib.ExitStack() as es:
        arena = es.enter_context(nc.sbuf_tensor("arena", [128, ARENA // 4], F32))
        ps = es.enter_context(nc.psum_tensor("ps", [128, 8 * 512], F32))
        sems = {n: es.enter_context(nc.semaphore(n)) for n in sem_names}
        block = es.enter_context(nc.Block())

        def f32v(off, n):
            return arena[:, off // 4: off // 4 + n]

        def bfv(off, n):
            return arena[:, off // 4: off // 4 + n // 2].bitcast(BF16)

        allbufs = []

        def mk(name, lo=None, n=None):
            b = Buf(name, lo, n)
            allbufs.append(b)
            return b

        oX, oU, BIG, MID = lay["X"], lay["U"], lay["BIG"], lay["MID"]
        X = f32v(oX, ND * TWB).rearrange("p (c t) -> p c t", c=ND)
        bX = [mk("x%d" % i, oX + i * TWB * 4, TWB * 4) for i in range(ND)]
        U = bfv(oU, ND * TWB).rearrange("p (c t) -> p c t", c=ND)
        bU = [mk("u%d" % i, oU + i * TWB * 2, TWB * 2) for i in range(ND)]
        ABUF = f32v(BIG, NCC * AW).rearrange("p (c t) -> p c t", c=NCC)
        bA = [mk("a%d" % i, BIG + i * AW * 4, AW * 4) for i in range(NCC)]
        o_cx = BIG + o_cx_rel
        CXB = f32v(o_cx, NCC * CW).rearrange("p (c t) -> p c t", c=NCC)
        bCX = [mk("cx%d" % i, o_cx + i * CW * 4, CW * 4) for i in range(NCC)]
        AACT = bfv(o_cx, NCC * TWB).rearrange("p (c t) -> p c t", c=NCC)
        bAACT = [mk("aact%d" % i, o_cx + i * TWB * 2, TWB * 2) for i in range(NCC)]
        YOUT = [f32v(o_cx + i * D * 4, D) for i in range(2)]
        bYOUT = [mk("yout%d" % i, o_cx + i * D * 4, D * 4) for i in range(2)]
        QT = bfv(BIG, ND * TWB).rearrange("p (c t) -> p c t", c=ND)
        bQT = [mk("qt%d" % i, BIG + i * TWB * 2, TWB * 2) for i in range(ND)]
        oOT = BIG + ND * TWB * 2
        OT = bfv(oOT, ND * TWB).rearrange("p (c t) -> p c t", c=ND)
        bOT = [mk("ot%d" % i, oOT + i * TWB * 2, TWB * 2) for i in range(ND)]
        oPT = BIG + 2 * ND * TWB * 2
        PT = bfv(oPT, 4 * TWB).rearrange("p (a m t) -> p a m t", a=2, m=2)
        bPT = [mk("pt%d" % i, oPT + i * 2 * TWB * 2, 2 * TWB * 2) for i in range(2)]
        HFF = bfv(BIG, NF * TWB).rearrange("p (c t) -> p c t", c=NF)
        bHFF = [mk("hff%d" % i, BIG + i * TWB * 2, TWB * 2) for i in range(NF)]
        KVSTG = [f32v(BIG + i * NM * D * 4, NM * D).rearrange("p (b f) -> p b f", b=NM) for i in range(2)]
        bKVSTG = [mk("kvstg%d" % i, BIG + i * NM * D * 4, NM * D * 4) for i in range(2)]
        XIN = [f32v(MID + i * D * 4, D) for i in range(NXIN)]
        bXIN = [mk("xin%d" % i, MID + i * D * 4, D * 4) for i in range(NXIN)]
        YB = f32v(MID, NCC * TWB).rearrange("p (c t) -> p c t", c=NCC)
        bY = [mk("y%d" % i, MID + i * TWB * 4, TWB * 4) for i in range(NCC)]
        assert ND * TWB * 2 <= NCC * TWB * 4
        MB = bfv(MID, ND * TWB).rearrange("p (c t) -> p c t", c=ND)
        bM = [mk("m%d" % i, MID + i * TWB * 2, TWB * 2) for i in range(ND)]
        oBBM = MID + NCC * TWB * 4
        BBM = bfv(oBBM, NCC * TWB).rearrange("p (c t) -> p c t", c=NCC)
        bBBM = [mk("bbm%d" % i, oBBM + i * TWB * 2, TWB * 2) for i in range(NCC)]
        T32 = [f32v(lay["T32"] + i * TWB * 4, TWB) for i in range(NT32)]
        bT32 = [mk("t32_%d" % i, lay["T32"] + i * TWB * 4, TWB * 4) for i in range(NT32)]
        TBF = [bfv(lay["TBF"] + i * TWB * 2, TWB) for i in range(4)]
        bTBF = [mk("tbf_%d" % i, lay["TBF"] + i * TWB * 2, TWB * 2) for i in range(4)]
        KT = bfv(lay["KT"], ND * c.NMEM).rearrange("p (c m) -> p c m", c=ND)
        VS = bfv(lay["VS"], NM * D).rearrange("p (b f) -> p b f", b=NM)
        bKT, bVS = mk("kt", lay["KT"], ND * c.NMEM * 2), mk("vs", lay["VS"], NM * D * 2)
        WS = [bfv(lay["WS"] + i * 4096, 2048) for i in range(c.NSLOT)]
        bWS = [mk("ws%d" % i, lay["WS"] + i * 4096, 4096) for i in range(c.NSLOT)]
        CVEC = f32v(lay["CVEC"], c.NV)
        IDENT = f32v(lay["IDENT"], 128)
        ONES = bfv(lay["ONES"], 128)
        CARA = f32v(lay["CARA"], NCC * c.PA).rearrange("p (c t) -> p c t", c=NCC)
        CARB = f32v(lay["CARB"], NCC * c.PB).rearrange("p (c t) -> p c t", c=NCC)
        MASK = f32v(lay["MASK"], 1)
        STST = f32v(lay["STST"], DC)
        CVST = f32v(lay["CVST"], 128)
        bCVEC, bIDENT, bONES, bMASK = mk("cvec"), mk("ident"), mk("ones"), mk("mask")
        bCARA, bCARB, bSTST, bCVST = mk("cara"), mk("carb"), mk("stst"), mk("cvst")
        bSTG = []
        bSCR = {key: mk("scr_%s_%d_%d" % key) for key in c.wtile}
        bPS = [mk("ps%d" % i) for i in range(8)]
        for b_ in bPS:
            b_.excl = True
        link_aliases(allbufs)

        state = dict(bank=0, stats=0, t32=0, tbf=0, ws=0, xin=0, yo=0, ring6=1)

        def next_bank():
            n = 6 if state["ring6"] else 5
            i = state["bank"] % n
            state["bank"] = (i + 1) % n
            return i

        def next_stats():
            i = state["stats"]
            state["stats"] = 1 - i
            return 6 + i

        def tmp32():
            i = state["t32"]
            state["t32"] = (i + 1) % NT32
            return T32[i], bT32[i]

        def tmpbf():
            i = state["tbf"]
            state["tbf"] = (i + 1) % 4
            return TBF[i], bTBF[i]

        def cv(col):
            return CVEC[:, col:col + 1]

        def pb(bi, W, c0=0):
            return ps[:, bi * 512 + c0: bi * 512 + c0 + W]

        wseq = []
        wm = dict(dry=True, i=0, issued=0, pending=None, fu=0, first={}, count={})

        def wview(sl):
            return WS[sl].rearrange("p (k n) -> p k n", n=128)

        def w_store_pending():
            pend = wm["pending"]
            if pend is None:
                return
            wm["pending"] = None
            sl, key, off, n = pend
            dst = wscr[off: off + 128 * n].rearrange("(p f) -> p f", p=128)
            src = WS[sl][:, 0:n]
            P.dma("sp", "wst%d" % sl, lambda e, dst=dst, src=src: e.dma_start(out=dst, in_=src),
                  reads=[bWS[sl]], writes=[bSCR[key]])

        def w_issue(i):
            key = wseq[i]
            name, jo, kg = key
            off, nk, k0 = c.wtile[key]
            sl = i % c.NSLOT
            n = nk * 128
            if wm["first"][key] == i:
                src32 = w32[off: off + 128 * n].rearrange("(p f) -> p f", p=128)
                dst = WS[sl][:, 0:n]
                P.dma("pool", "wc%d" % sl, lambda e, dst=dst, src32=src32: e.dma_start(out=dst, in_=src32),
                      reads=[], writes=[bWS[sl]])
                w_store_pending()
                if wm["count"][key] > 1:
                    wm["pending"] = (sl, key, off, n)
            else:
                src = wscr[off: off + 128 * n].rearrange("(p f) -> p f", p=128)
                dst = WS[sl][:, 0:n]
                P.dma("sp", "ws%d" % sl, lambda e, dst=dst, src=src: e.dma_start(out=dst, in_=src),
                      reads=[bSCR[key]], writes=[bWS[sl]])
                w_store_pending()

        def wload(name, jo, kg=0):
            key = (name, jo, kg)
            nk = c.wtile[key][1]
            if wm["dry"]:
                wseq.append(key)
                return wview(0), bWS[0], nk
            i = wm["i"]
            wm["i"] += 1
            assert wseq[i] == key, (i, wseq[i], key)
            while wm["issued"] < min(len(wseq), i + 1 + AHEAD):
                w_issue(wm["issued"])
                wm["issued"] += 1
            sl = i % c.NSLOT
            return wview(sl), bWS[sl], nk

        def mm_group(bi, pairs, col0, ncol, all_sig=False):
            n = len(pairs)
            out = pb(bi, ncol, col0)
            for i, (l, r, rb) in enumerate(pairs):
                P.op("pe", lambda e, out=out, l=l, r=r, i=i: e.matmul(out, lhsT=l, rhs=r, start=(i == 0), stop=(i == n - 1)),
                     reads=rb, writes=[bPS[bi]], sig=(all_sig or i == n - 1))

        def proj_ws(name, jo, act, bact, W, nkg=1):
            bi = next_bank()
            pairs = []
            for kg in range(nkg):
                wt, bw, nk = wload(name, jo, kg)
                k0 = c.wtile[(name, jo, kg)][2]
                for k in range(nk):
                    pairs.append((wt[:, k, :], act[:, k0 + k, 0:W], [bw, bact[k0 + k]]))
            mm_group(bi, pairs, 0, W)
            return bi

        def rms_stat_mm(st, W, k, sq, bsq):
            P.op("pe", lambda e: e.matmul(pb(st, W), lhsT=ONES, rhs=sq[:, 0:W], start=(k == 0), stop=(k == ND - 1)),
                 reads=[bsq, bONES], writes=[bPS[st]], sig=True)

        def rms_square(W, k):
            sq, bsq = tmpbf()
            P.op("act", lambda e: e.activation(out=sq[:, 0:W], in_=X[:, k, 0:W], func=AF.Square),
                 reads=[bX[k]], writes=[bsq])
            return sq, bsq

        def rms_finish(st, W, gcol, to_x=False):
            sd, bsd = tmp32()
            P.op("act", lambda e: e.activation(out=sd[:, 0:W], in_=pb(st, W), func=AF.Ln, scale=1.0 / D, bias=RMS_EPS),
                 reads=[bPS[st]], writes=[bsd])
            P.op("act", lambda e: e.activation(out=pb(st, W), in_=sd[:, 0:W], func=AF.Exp, scale=-0.5),
                 reads=[bsd], writes=[bPS[st]])
            for k in range(ND):
                dst = X[:, k, 0:W] if to_x else U[:, k, 0:W]
                P.op("dve", lambda e, k=k, dst=dst: e.scalar_tensor_tensor(
                        out=dst, in0=X[:, k, 0:W], scalar=cv(gcol + k), in1=pb(st, W), op0=ALU.mult, op1=ALU.mult),
                     reads=[bX[k], bPS[st], bCVEC], writes=[bX[k]] if to_x else [bU[k]])

        def rmsnorm(W, gcol, to_x=False):
            st = next_stats()
            for k in range(ND):
                sq, bsq = rms_square(W, k)
                rms_stat_mm(st, W, k, sq, bsq)
            rms_finish(st, W, gcol, to_x)

        def residual_phase(name, act, bact, W, nkg=1):
            st = next_stats()
            pend = []
            for jo in range(ND):
                b = proj_ws(name, jo, act, bact, W, nkg=nkg)
                P.op("dve", lambda e, b=b, jo=jo: e.tensor_tensor(out=X[:, jo, 0:W], in0=pb(b, W), in1=X[:, jo, 0:W], op=ALU.add),
                     reads=[bPS[b], bX[jo]], writes=[bX[jo]])
                sq, bsq = rms_square(W, jo)
                pend.append((jo, sq, bsq))
                if len(pend) > 2:
                    k, sq0, bsq0 = pend.pop(0)
                    rms_stat_mm(st, W, k, sq0, bsq0)
            for k, sq0, bsq0 in pend:
                rms_stat_mm(st, W, k, sq0, bsq0)
            return st

        def xin_dma(src, row, n, eng="pool"):
            s = state["xin"]
            state["xin"] = (s + 1) % NXIN
            dst = XIN[s][0:n, :]
            sr = src[row: row + n, :]
            P.dma(eng, "xin%d" % s, lambda e, dst=dst, sr=sr: e.dma_start(out=dst, in_=sr),
                  reads=[], writes=[bXIN[s]])
            return s

        def load_rows(src, row0, W, eng="pool"):
            blocks, slots, deferred = [], [], []
            r = 0
            while r < W:
                n = min(128, W - r)
                if len(blocks) < NXIN:
                    slots.append(xin_dma(src, row0 + r, n, eng))
                    blocks.append((r, n))
                else:
                    deferred.append((r, n, row0 + r))
                r += n
            return blocks, slots, deferred, src

        def transpose_in(blocks, slots, deferred=(), src=None):
            blocks, slots, deferred = list(blocks), list(slots), list(deferred)
            i = 0
            while i < len(blocks):
                (r, n), s = blocks[i], slots[i]
                i += 1
                for g in range(0, ND, 4):
                    bi = next_bank()
                    ng = min(4, ND - g)
                    for q in range(ng):
                        k = g + q
                        P.op("pe", lambda e, bi=bi, q=q, k=k, s=s, n=n: e.transpose(
                                pb(bi, n, q * 128), XIN[s][0:n, k * 128:(k + 1) * 128], IDENT[0:n, 0:n]),
                             reads=[bXIN[s], bIDENT], writes=[bPS[bi]], sig=(q == ng - 1))
                    srcp = pb(bi, ng * 128).rearrange("p (q t) -> p q t", q=ng)[:, :, 0:n]
                    dst = X[:, g:g + ng, r:r + n]
                    P.op("act", lambda e, srcp=srcp, dst=dst: e.activation(out=dst, in_=srcp, func=AF.Identity),
                         reads=[bPS[bi]], writes=[bX[g + q] for q in range(ng)])
                if deferred:
                    r2, n2, row2 = deferred.pop(0)
                    slots.append(xin_dma(src, row2, n2))
                    blocks.append((r2, n2))

        def state_in(src_ap, npre, car, bcar):
            P.dma("pool", "sti", lambda e: e.dma_start(out=STST[0:npre, :], in_=src_ap), reads=[], writes=[bSTST])
            bi = next_bank()
            for k in range(NCC):
                P.op("pe", lambda e, k=k, bi=bi: e.transpose(pb(bi, npre, k * npre), STST[0:npre, k * 128:(k + 1) * 128],
                                                           IDENT[0:npre, 0:npre]),
                     reads=[bSTST, bIDENT], writes=[bPS[bi]], sig=(k == NCC - 1))
            P.op("act", lambda e, bi=bi: e.activation(
                    out=car, in_=pb(bi, NCC * npre).rearrange("p (c t) -> p c t", c=NCC), func=AF.Identity),
                 reads=[bPS[bi]], writes=[bcar])

        def state_out(car, bcar, npre, dst_ap):
            for g in range(0, NCC, 4):
                bi = next_bank()
                ng = min(4, NCC - g)
                for q in range(ng):
                    k = g + q
                    P.op("pe", lambda e, bi=bi, q=q, k=k: e.transpose(
                            ps[0:npre, bi * 512 + q * 128: bi * 512 + (q + 1) * 128], car[:, k, :], IDENT),
                         reads=[bcar, bIDENT], writes=[bPS[bi]], sig=(q == ng - 1))
                P.op("act", lambda e, bi=bi, g=g, ng=ng: e.activation(
                        out=STST[0:npre, g * 128:(g + ng) * 128], in_=ps[0:npre, bi * 512: bi * 512 + ng * 128],
                        func=AF.Identity),
                     reads=[bPS[bi]], writes=[bSTST])
            P.dma("pool", "so", lambda e: e.dma_start(out=dst_ap, in_=STST[0:npre, :]), reads=[bSTST], writes=[])

        def stage(k):
            if getattr(cfg, "stop", None) is not None and k > cfg.stop:
                raise _Stop()

        def program():
          if True:
            P.dma("sp", "cst", lambda e: e.dma_start(out=IDENT, in_=id_in[:, :]), reads=[], writes=[bIDENT])
            P.dma("sp", "cst", lambda e: e.dma_start(out=MASK, in_=mk_in[:, :]), reads=[], writes=[bMASK])
            tot = P.dcnt["cst"]
            bIDENT.w = [(Ev("cst", tot), "dma")]
            bMASK.w = [(Ev("cst", tot), "dma")]
            P.op("pool", lambda e: e.memset(ONES, 1.0), reads=[], writes=[bONES])
            r0 = 0
            while r0 < c.NV:
                n = min(128, c.NV - r0)
                P.dma("sp", "cvl", lambda e, r0=r0, n=n: e.dma_start(out=CVST[0:n, :], in_=cv_in[r0:r0 + n, :]),
                      reads=[], writes=[bCVST])
                bi = next_bank()
                P.op("pe", lambda e, bi=bi, n=n: e.transpose(pb(bi, n), CVST[0:n, :], IDENT[0:n, 0:n]),
                     reads=[bCVST, bIDENT], writes=[bPS[bi]])
                P.op("act", lambda e, bi=bi, n=n, r0=r0: e.activation(out=CVEC[:, r0:r0 + n], in_=pb(bi, n), func=AF.Identity),
                     reads=[bPS[bi]], writes=[bCVEC])
                r0 += n

            scale_att = float(c.HD) ** -0.5

            def mixer(W, tile):
                halo = tile["halo"]
                sb = 4 + 10 * tile.get("ti", 0)
                if halo:
                    for j in range(NCC):
                        P.op("dve", lambda e, j=j: e.memset(ABUF[:, j, 0:c.PA], 0.0), reads=[], writes=[bA[j]])
                        P.op("dve", lambda e, j=j: e.memset(CXB[:, j, 0:c.PB], 0.0), reads=[], writes=[bCX[j]])
                else:
                    P.op("pool", lambda e: e.tensor_copy(out=ABUF[:, :, 0:c.PA], in_=CARA), reads=[bCARA], writes=bA)
                    P.op("pool", lambda e: e.tensor_copy(out=CXB[:, :, 0:c.PB], in_=CARB), reads=[bCARB], writes=bCX)
                stage(sb + 0.1)
                rmsnorm(W, c.V_NMIX)
                stage(sb + 0.2)
                s1, s2 = 6, 7
                state["ring6"] = 0
                for j in range(NCC):
                    bv = proj_ws("w_in", c.C_AV + j, U, bU, W)
                    bg = proj_ws("w_in", c.C_AG + j, U, bU, W)
                    sg, bsg = tmp32()
                    P.op("act", lambda e, bg=bg, sg=sg, j=j: e.activation(out=sg[:, 0:W], in_=pb(bg, W), func=AF.Sigmoid,
                                                                          bias=cv(c.V_BIN + c.C_AG + j)),
                         reads=[bPS[bg], bCVEC], writes=[bsg])
                    P.op("dve", lambda e, bv=bv, sg=sg, j=j: e.scalar_tensor_tensor(
                            out=ABUF[:, j, c.PA:c.PA + W], in0=pb(bv, W), scalar=cv(c.V_BIN + c.C_AV + j),
                            in1=sg[:, 0:W], op0=ALU.add, op1=ALU.mult),
                         reads=[bPS[bv], bsg, bCVEC], writes=[bA[j]])
                    if halo:
                        P.op("dve", lambda e, j=j: e.tensor_scalar(out=ABUF[:, j, c.PA:c.PA + halo], in0=ABUF[:, j, c.PA:c.PA + halo],
                                                                   scalar1=MASK, scalar2=None, op0=ALU.mult),
                             reads=[bA[j], bMASK], writes=[bA[j]])
                    bc_ = proj_ws("w_in", c.C_BC + j, U, bU, W)
                    bx_ = proj_ws("w_in", c.C_BX + j, U, bU, W)
                    bb_ = proj_ws("w_in", c.C_BB + j, U, bU, W)
                    xs, bxs = tmp32()
                    P.op("act", lambda e, bx_=bx_, xs=xs, j=j: e.activation(out=xs[:, 0:W], in_=pb(bx_, W), func=AF.Identity,
                                                                            bias=cv(c.V_BIN + c.C_BX + j)),
                         reads=[bPS[bx_], bCVEC], writes=[bxs])
                    cs, bcs = tmp32()
                    P.op("act", lambda e, bc_=bc_, cs=cs, j=j: e.activation(out=cs[:, 0:W], in_=pb(bc_, W), func=AF.Identity,
                                                                            bias=cv(c.V_BIN + c.C_BC + j)),
                         reads=[bPS[bc_], bCVEC], writes=[bcs])
                    P.op("pool", lambda e, cs=cs, xs=xs, j=j: e.tensor_tensor(out=CXB[:, j, c.PB:c.PB + W], in0=cs[:, 0:W],
                                                                              in1=xs[:, 0:W], op=ALU.mult),
                         reads=[bcs, bxs], writes=[bCX[j]])
                    if halo:
                        P.op("dve", lambda e, j=j: e.tensor_scalar(out=CXB[:, j, c.PB:c.PB + halo], in0=CXB[:, j, c.PB:c.PB + halo],
                                                                   scalar1=MASK, scalar2=None, op0=ALU.mult),
                             reads=[bCX[j], bMASK], writes=[bCX[j]])
                    cb, bcb = tmp32()
                    P.op("pool", lambda e, cb=cb, j=j: e.tensor_scalar(out=cb[:, 0:W], in0=CXB[:, j, 0:W],
                                                                       scalar1=cv(c.V_CBW + j), scalar2=None, op0=ALU.mult),
                         reads=[bCX[j], bCVEC], writes=[bcb])
                    for t in range(1, c.KB):
                        tt, btt = tmp32()
                        P.op("pool", lambda e, tt=tt, j=j, t=t: e.tensor_scalar(out=tt[:, 0:W], in0=CXB[:, j, t:t + W],
                                                                                scalar1=cv(c.V_CBW + t * NCC + j), scalar2=None, op0=ALU.mult),
                             reads=[bCX[j], bCVEC], writes=[btt])
                        P.op("pool", lambda e, tt=tt, cb=cb: e.tensor_tensor(out=cb[:, 0:W], in0=cb[:, 0:W], in1=tt[:, 0:W], op=ALU.add),
                             reads=[bcb, btt], writes=[bcb])
                    bs, bbs = tmp32()
                    P.op("act", lambda e, bb_=bb_, bs=bs, j=j: e.activation(out=bs[:, 0:W], in_=pb(bb_, W), func=AF.Identity,
                                                                            bias=cv(c.V_BIN + c.C_BB + j)),
                         reads=[bPS[bb_], bCVEC], writes=[bbs])
                    P.op("pool", lambda e, bs=bs, cb=cb, j=j: e.tensor_tensor(out=BBM[:, j, 0:W], in0=bs[:, 0:W], in1=cb[:, 0:W], op=ALU.mult),
                         reads=[bbs, bcb], writes=[bBBM[j]])
                    acc = pb(5, W)
                    P.op("dve", lambda e, j=j, acc=acc: e.tensor_scalar(out=acc, in0=ABUF[:, j, 0:W], scalar1=cv(c.V_CAW + j),
                                                                        scalar2=cv(c.V_CAB + j), op0=ALU.mult, op1=ALU.add),
                         reads=[bA[j], bCVEC], writes=[bPS[5]])
                    for t in range(1, c.KA):
                        P.op("dve", lambda e, j=j, t=t, acc=acc: e.scalar_tensor_tensor(
                                out=acc, in0=ABUF[:, j, t:t + W], scalar=cv(c.V_CAW + t * NCC + j), in1=acc,
                                op0=ALU.mult, op1=ALU.add),
                             reads=[bA[j], bPS[5], bCVEC], writes=[bPS[5]])
                    P.op("act", lambda e, j=j, acc=acc: e.activation(out=YB[:, j, 0:W], in_=acc, func=AF.Identity),
                         reads=[bPS[5]], writes=[bY[j]])
                    yb, byb = tmpbf()
                    P.op("act", lambda e, yb=yb, acc=acc: e.activation(out=yb[:, 0:W], in_=acc, func=AF.Identity),
                         reads=[bPS[5]], writes=[byb])
                    sq, bsq = tmpbf()
                    P.op("act", lambda e, sq=sq, acc=acc: e.activation(out=sq[:, 0:W], in_=acc, func=AF.Square),
                         reads=[bPS[5]], writes=[bsq])
                    if j >= 1:
                        pend()
                    def pend(yb=yb, sq=sq, byb=byb, bsq=bsq, j=j):
                        P.op("pe", lambda e: e.matmul(pb(s1, W), lhsT=ONES, rhs=yb[:, 0:W], start=(j == 0), stop=(j == NCC - 1)),
                             reads=[byb, bONES], writes=[bPS[s1]], sig=True)
                        P.op("pe", lambda e: e.matmul(pb(s2, W), lhsT=ONES, rhs=sq[:, 0:W], start=(j == 0), stop=(j == NCC - 1)),
                             reads=[bsq, bONES], writes=[bPS[s2]], sig=True)
                pend()
                state["ring6"] = 1
                P.op("pool", lambda e: e.tensor_copy(out=CARA, in_=ABUF[:, :, W:W + c.PA]), reads=bA, writes=[bCARA])
                P.op("pool", lambda e: e.tensor_copy(out=CARB, in_=CXB[:, :, W:W + c.PB]), reads=bCX, writes=[bCARB])
                stage(sb + 0.6)
                mean, bmean = tmp32()
                msq, bmsq = tmp32()
                var, bvar = tmp32()
                P.op("dve", lambda e: e.tensor_scalar(out=mean[:, 0:W], in0=pb(s1, W), scalar1=1.0 / DC, scalar2=None, op0=ALU.mult),
                     reads=[bPS[s1]], writes=[bmean])
                stage(sb + 0.61)
                P.op("act", lambda e: e.activation(out=msq[:, 0:W], in_=pb(s1, W), func=AF.Square, scale=1.0 / DC),
                     reads=[bPS[s1]], writes=[bmsq])
                stage(sb + 0.62)
                P.op("dve", lambda e: e.scalar_tensor_tensor(out=var[:, 0:W], in0=pb(s2, W), scalar=1.0 / DC, in1=msq[:, 0:W],
                                                             op0=ALU.mult, op1=ALU.subtract),
                     reads=[bPS[s2], bmsq], writes=[bvar])
                stage(sb + 0.63)
                P.op("act", lambda e: e.activation(out=var[:, 0:W], in_=var[:, 0:W], func=AF.Ln, bias=LN_EPS),
                     reads=[bvar], writes=[bvar])
                stage(sb + 0.64)
                P.op("act", lambda e: e.activation(out=pb(s2, W), in_=var[:, 0:W], func=AF.Exp, scale=-0.5),
                     reads=[bvar], writes=[bPS[s2]])
                stage(sb + 0.65)
                for j in range(NCC):
                    t_, bt_ = (msq, bmsq) if j % 2 == 0 else (var, bvar)
                    P.op("pool", lambda e, j=j, t_=t_: e.tensor_tensor(out=t_[:, 0:W], in0=YB[:, j, 0:W], in1=mean[:, 0:W],
                                                                       op=ALU.subtract),
                         reads=[bY[j], bmean], writes=[bt_])
                    stage(sb + 0.66)
                    P.op("dve", lambda e, t_=t_: e.tensor_tensor(out=t_[:, 0:W], in0=pb(s2, W), in1=t_[:, 0:W], op=ALU.mult),
                         reads=[bPS[s2], bt_], writes=[bt_])
                    stage(sb + 0.67)
                    P.op("act", lambda e, j=j, t_=t_: e.activation(out=AACT[:, j, 0:W], in_=t_[:, 0:W], func=AF.Silu,
                                                                   scale=cv(c.V_LNG + j), bias=cv(c.V_LNB + j)),
                         reads=[bt_, bCVEC], writes=[bAACT[j]])
                stage(sb + 0.7)
                def gates(jo):
                    b2 = proj_ws("w_in", c.C_GA + jo, U, bU, W)
                    b4 = proj_ws("w_in", c.C_GB + jo, U, bU, W)
                    b3 = proj_ws("w_b_out", jo, BBM, bBBM, W)
                    sA, bsA = tmp32()
                    sB, bsB = tmp32()
                    P.op("act", lambda e: e.activation(out=sA[:, 0:W], in_=pb(b2, W), func=AF.Sigmoid,
                                                       bias=cv(c.V_BIN + c.C_GA + jo)),
                         reads=[bPS[b2], bCVEC], writes=[bsA])
                    P.op("act", lambda e: e.activation(out=sB[:, 0:W], in_=pb(b4, W), func=AF.Sigmoid,
                                                       bias=cv(c.V_BIN + c.C_GB + jo)),
                         reads=[bPS[b4], bCVEC], writes=[bsB])
                    P.op("dve", lambda e: e.tensor_tensor(out=sB[:, 0:W], in0=pb(b3, W), in1=sB[:, 0:W], op=ALU.mult),
                         reads=[bPS[b3], bsB], writes=[bsB])
                    return sA, bsA, sB, bsB

                LA = 2
                pend_g = [gates(j_) for j_ in range(min(LA, ND))]
                for jo in range(ND):
                    if jo + LA < ND:
                        pend_g.append(gates(jo + LA))
                    b1 = proj_ws("w_a_out", jo, AACT, bAACT, W)
                    sA, bsA, sB, bsB = pend_g.pop(0)
                    P.op("dve", lambda e, b1=b1, sA=sA, jo=jo: e.scalar_tensor_tensor(
                            out=sA[:, 0:W], in0=pb(b1, W), scalar=cv(c.V_BAO + jo), in1=sA[:, 0:W], op0=ALU.add, op1=ALU.mult),
                         reads=[bPS[b1], bsA, bCVEC], writes=[bsA])
                    P.op("pool", lambda e, sA=sA, sB=sB, jo=jo: e.tensor_tensor(out=MB[:, jo, 0:W], in0=sA[:, 0:W], in1=sB[:, 0:W],
                                                                                op=ALU.add),
                         reads=[bsA, bsB], writes=[bM[jo]])
                stage(sb + 0.8)
                return residual_phase("w_mix_out", MB, bM, W)

            def attention(W, st_in):
                rms_finish(st_in, W, c.V_NX)
                for jo in range(ND):
                    b = proj_ws("w_q", jo, U, bU, W)
                    P.op("act", lambda e, b=b, jo=jo: e.activation(out=QT[:, jo, 0:W], in_=pb(b, W), func=AF.Identity),
                         reads=[bPS[b]], writes=[bQT[jo]])
                def scores(h):
                    par = h % 2
                    for mb in range(NM):
                        bi = next_bank()
                        pairs = [(KT[:, h * c.HC + dc, mb * 128:(mb + 1) * 128], QT[:, h * c.HC + dc, 0:W], [bKT, bQT[h * c.HC + dc]])
                                 for dc in range(c.HC)]
                        mm_group(bi, pairs, 0, W)
                        P.op("act", lambda e, bi=bi, par=par, mb=mb: e.activation(out=PT[:, par, mb, 0:W], in_=pb(bi, W),
                                                                                  func=AF.Exp, scale=scale_att),
                             reads=[bPS[bi]], writes=[bPT[par]])

                scores(0)
                for h in range(c.NH):
                    par = h % 2
                    if h + 1 < c.NH:
                        scores(h + 1)
                    st = next_stats()
                    mm_group(st, [(ONES, PT[:, par, mb, 0:W], [bPT[par], bONES]) for mb in range(NM)], 0, W)
                    rd, brd = tmp32()
                    P.op("dve", lambda e, st=st, rd=rd: e.reciprocal(out=rd[:, 0:W], in_=pb(st, W)), reads=[bPS[st]], writes=[brd])
                    for dc in range(c.HC):
                        bi = next_bank()
                        col = h * c.HD + dc * 128
                        pairs = [(VS[:, mb, col:col + 128], PT[:, par, mb, 0:W], [bVS, bPT[par]]) for mb in range(NM)]
                        mm_group(bi, pairs, 0, W)
                        P.op("dve", lambda e, bi=bi, rd=rd, k=h * c.HC + dc: e.tensor_tensor(
                                out=OT[:, k, 0:W], in0=pb(bi, W), in1=rd[:, 0:W], op=ALU.mult),
                             reads=[bPS[bi], brd], writes=[bOT[h * c.HC + dc]])
                return residual_phase("w_o", OT, bOT, W)

            def ffn(W, st_in):
                rms_finish(st_in, W, c.V_NFFN)
                for f in range(NF):
                    bg = proj_ws("w_gate", f, U, bU, W)
                    bu = proj_ws("w_up", f, U, bU, W)
                    sg, bsg = tmp32()
                    P.op("act", lambda e, bg=bg, sg=sg: e.activation(out=sg[:, 0:W], in_=pb(bg, W), func=AF.Silu),
                         reads=[bPS[bg]], writes=[bsg])
                    P.op("dve", lambda e, bu=bu, sg=sg, f=f: e.tensor_tensor(out=HFF[:, f, 0:W], in0=pb(bu, W), in1=sg[:, 0:W], op=ALU.mult),
                         reads=[bPS[bu], bsg], writes=[bHFF[f]])
                return residual_phase("w_down", HFF, bHFF, W, nkg=c.nkg["w_down"])

            def final_out(W, tile, st_in):
                rms_finish(st_in, W, c.V_NFIN, to_x=True)
                for (c0, n_out, orow) in tile["outs"]:
                    r = 0
                    while r < n_out:
                        n = min(128, n_out - r)
                        s = state["yo"]
                        state["yo"] = 1 - s
                        for g in range(0, ND, 4):
                            bi = next_bank()
                            ng = min(4, ND - g)
                            for q in range(ng):
                                k = g + q
                                P.op("pe", lambda e, bi=bi, q=q, k=k, n=n, cc=c0 + r: e.transpose(
                                        ps[0:n, bi * 512 + q * 128: bi * 512 + (q + 1) * 128], X[:, k, cc:cc + n], IDENT),
                                     reads=[bX[k], bIDENT], writes=[bPS[bi]], sig=(q == ng - 1))
                            P.op("act", lambda e, bi=bi, g=g, ng=ng, n=n, s=s: e.activation(
                                    out=YOUT[s][0:n, g * 128:(g + ng) * 128], in_=ps[0:n, bi * 512: bi * 512 + ng * 128], func=AF.Identity),
                                 reads=[bPS[bi]], writes=[bYOUT[s]])
                        dst = yall[orow + r: orow + r + n, :]
                        P.dma("pool", "yo%d" % s, lambda e, dst=dst, s=s, n=n: e.dma_start(out=dst, in_=YOUT[s][0:n, :]),
                              reads=[bYOUT[s]], writes=[])
                        r += n

            def kv_from_cache():
                for i, src in enumerate((kc_in, vc_in)):
                    P.dma("pool", "kvi%d" % i, lambda e, i=i, src=src: e.dma_start(
                            out=KVSTG[i], in_=src.rearrange("(b p) f -> p b f", p=128)),
                          reads=[], writes=[bKVSTG[i]])
                per = 512 // (NM * 128)
                for k in range(0, ND, per):
                    bi = next_bank()
                    nq = min(per, ND - k)
                    for q in range(nq):
                        for mb in range(NM):
                            P.op("pe", lambda e, bi=bi, q=q, mb=mb, k=k: e.transpose(
                                    pb(bi, 128, (q * NM + mb) * 128), KVSTG[0][:, mb, (k + q) * 128:(k + q + 1) * 128], IDENT),
                                 reads=[bKVSTG[0], bIDENT], writes=[bPS[bi]], sig=(q == nq - 1 and mb == NM - 1))
                    P.op("act", lambda e, bi=bi, k=k, nq=nq: e.activation(
                            out=KT[:, k:k + nq, :], in_=pb(bi, nq * NM * 128).rearrange("p (q m) -> p q m", q=nq), func=AF.Identity),
                         reads=[bPS[bi]], writes=[bKT])
                for mb in range(NM):
                    P.op("act", lambda e, mb=mb: e.activation(out=VS[:, mb, :], in_=KVSTG[1][:, mb, :], func=AF.Identity),
                         reads=[bKVSTG[1]], writes=[bVS])

            def kv_from_mem():
                W = c.NMEM
                transpose_in(*load_rows(memb, 0, W))
                rmsnorm(W, c.V_NMEM)
                stage(9.1)
                for wi, (wname, dst_o) in enumerate((("w_k", nk_o), ("w_v", nv_o))):
                    stage(9.2 + 0.4 * wi)
                    for jo in range(ND):
                        wt, bw, nk = wload(wname, jo, 0)
                        if wi == 0:
                            bi = next_bank()
                            mm_group(bi, [(wt[:, k, :], U[:, k, 0:W], [bw, bU[k]]) for k in range(ND)], 0, W)
                            P.op("act", lambda e, bi=bi, jo=jo: e.activation(out=KT[:, jo, :], in_=pb(bi, W), func=AF.Identity),
                                 reads=[bPS[bi]], writes=[bKT])
                        bi = next_bank()
                        for mb in range(NM):
                            mm_group(bi, [(U[:, k, mb * 128:(mb + 1) * 128], wt[:, k, :], [bw, bU[k]]) for k in range(ND)], mb * 128, 128)
                        src = pb(bi, NM * 128).rearrange("p (b n) -> p b n", b=NM)
                        P.op("act", lambda e, src=src, wi=wi, jo=jo: e.activation(out=KVSTG[wi][:, :, jo * 128:(jo + 1) * 128], in_=src,
                                                                                  func=AF.Identity),
                             reads=[bPS[bi]], writes=[bKVSTG[wi]])
                        if wi == 1:
                            P.op("act", lambda e, src=src, jo=jo: e.activation(out=VS[:, :, jo * 128:(jo + 1) * 128], in_=src,
                                                                               func=AF.Identity),
                                 reads=[bPS[bi]], writes=[bVS])
                    stage(9.3 + 0.4 * wi)
                    P.dma("pool", "kvo%d" % wi, lambda e, wi=wi, dst_o=dst_o: e.dma_start(
                            out=dst_o.rearrange("(b p) f -> p b f", p=128), in_=KVSTG[wi]),
                          reads=[bKVSTG[wi]], writes=[])

            stage(2)
            tiles = c.tiles
            kv_from_mem()
            pre = load_rows(xall, tiles[0]["row0"], tiles[0]["W"])
            for ti, tile in enumerate(tiles):
                W = tile["W"]
                tile["ti"] = ti
                if tile["kind"] == "sample":
                    state_out(CARA, bCARA, c.PA, nca[0])
                    state_out(CARB, bCARB, c.PB, ncb[0])
                    state_in(sa_in[:, :], c.PA, CARA, bCARA)
                    state_in(sb_in[:, :], c.PB, CARB, bCARB)
                stage(3 + 10 * ti)
                transpose_in(*pre)
                stage(4 + 10 * ti)
                st_ = mixer(W, tile)
                stage(5 + 10 * ti)
                if ti + 1 < len(tiles):
                    nt_ = tiles[ti + 1]
                    pre = load_rows(xall, nt_["row0"], nt_["W"])
                stage(6 + 10 * ti)
                st_ = attention(W, st_)
                if ti + 1 < len(tiles) and tiles[ti + 1]["kind"] == "sample":
                    kv_from_cache()
                stage(7 + 10 * ti)
                st_ = ffn(W, st_)
                stage(8 + 10 * ti)
                final_out(W, tile, st_)
                stage(9 + 10 * ti)
            state_out(CARA, bCARA, c.PA, nca[1])
            state_out(CARB, bCARB, c.PB, ncb[1])
            w_store_pending()

        def reset_tracking():
            P.__init__()
            for b_ in allbufs:
                b_.w = []
                b_.r = {}
            for k_ in state:
                state[k_] = 0
            state["ring6"] = 1

        saved_stop = getattr(cfg, "stop", None)
        cfg.stop = None
        program()
        cfg.stop = saved_stop
        for i_, key_ in enumerate(wseq):
            wm["first"].setdefault(key_, i_)
            wm["count"][key_] = wm["count"].get(key_, 0) + 1
        wm["dry"] = False
        reset_tracking()
        try:
            program()
        except _Stop:
            pass
        P.wait_all("pool", ["yo0", "yo1", "so", "kvo0", "kvo1"])
        P.wait_all("sp", ["cst", "cvl", "sg0", "sg1"] + ["ws%d" % i for i in range(c.NSLOT)] + ["wst%d" % i for i in range(c.NSLOT)])
        P.wait_all("pool", ["sti", "kvi0", "kvi1", "xin0", "xin1", "xin2"] + ["wc%d" % i for i in range(c.NSLOT)])

        def replay(eng_name):
            def run(e):
                for item in P.q[eng_name]:
                    if item[0] == "wait":
                        e.wait_ge(sems[item[1]], item[2])
                    else:
                        ins = item[1](e)
                        if item[2] is not None:
                            ins.then_inc(sems[item[2]], item[3])
            return run

        block.sync(replay("sp"))
        block.tensor(replay("pe"))
        block.scalar(replay("act"))
        block.vector(replay("dve"))
        block.gpsimd(replay("pool"))
        cfg.stats = {e: len(P.q[e]) for e in P.ENGS}
    return nc


def make_in_maps(cfg, inputs):
    c = cfg
    f = lambda a: np.ascontiguousarray(np.asarray(a, dtype=np.float32))
    xp = f(inputs["x_prompt"])
    xs = f(inputs["x_sample"])
    B, S, D = xp.shape
    segs = S // c.NP
    vec_names = ["norm_mix_g", "b_in", "conv_a_w", "conv_a_b", "ln_a_g", "ln_a_b", "b_a_out", "conv_b_w",
                 "norm_x_g", "norm_mem_g", "norm_ffn_g", "norm_final_g"]
    cvec = np.concatenate([f(inputs[n]).reshape(-1, 128) for n in vec_names], axis=0)
    assert cvec.shape[0] == c.NV, (cvec.shape, c.NV)
    ident = np.eye(128, dtype=np.float32)
    w32 = np.empty((c.WSCR,), np.float32)
    for name, K, N in c.weights:
        Wm = f(inputs[name])[0]
        nkc, njo = K // 128, N // 128
        W4 = Wm.reshape(nkc, 128, njo, 128)
        for (nm, jo, kg), (off, nk, k0) in c.wtile.items():
            if nm != name:
                continue
            t_ = W4[k0:k0 + nk, :, jo, :].transpose(1, 0, 2)
            w32[off: off + 128 * nk * 128] = t_.reshape(-1)
    in_maps = []
    for core in range(NCORES):
        b, s = core // segs, core % segs
        t0 = s * c.NP
        xall = np.empty((c.NROWS, D), np.float32)
        if s == 0:
            xall[0:c.HALO] = 0.0
        else:
            xall[0:c.HALO] = xp[b, t0 - c.HALO:t0]
        xall[c.HALO:c.HALO + c.NP] = xp[b, t0:t0 + c.NP]
        xall[c.HALO + c.NP:] = xs[core]
        m = dict(
            xall=xall,
            memb=f(inputs["mem_prompt"])[b],
            kcache=f(inputs["cache_mem_k"])[0, core].reshape(c.NMEM, D),
            vcache=f(inputs["cache_mem_v"])[0, core].reshape(c.NMEM, D),
            sta=f(inputs["state_conv_a"])[0, core],
            stb=f(inputs["state_conv_b"])[0, core],
            cvec=cvec,
            ident=ident,
            mask=np.full((128, 1), 0.0 if s == 0 else 1.0, np.float32),
        )
        m["w32"] = w32
        in_maps.append(m)
    return in_maps, (B, S, D, segs)


def assemble(cfg, res, meta):
    c = cfg
    B, S, D, segs = meta
    y_prompt = np.empty((B, S, D), np.float32)
    y_sample = np.empty((NCORES, c.NS, D), np.float32)
    nca_p = np.empty((1, B, c.PA, c.DC), np.float32)
    ncb_p = np.empty((1, B, c.PB, c.DC), np.float32)
    nk_p = np.empty((1, B, c.NMEM, c.NH, c.HD), np.float32)
    nv_p = np.empty((1, B, c.NMEM, c.NH, c.HD), np.float32)
    nca_s = np.empty((1, NCORES, c.PA, c.DC), np.float32)
    ncb_s = np.empty((1, NCORES, c.PB, c.DC), np.float32)
    for core in range(NCORES):
        r = res[core]
        b, s = core // segs, core % segs
        y_prompt[b, s * c.NP:(s + 1) * c.NP] = r["yall"][0:c.NP]
        y_sample[core] = r["yall"][c.NP:]
        nca_s[0, core] = r["nca"][1]
        ncb_s[0, core] = r["ncb"][1]
        if s == segs - 1:
            nca_p[0, b] = r["nca"][0]
            ncb_p[0, b] = r["ncb"][0]
        if s == 0:
            nk_p[0, b] = r["nk"].reshape(c.NMEM, c.NH, c.HD)
            nv_p[0, b] = r["nv"].reshape(c.NMEM, c.NH, c.HD)
    return (y_prompt, y_sample, nca_p, ncb_p, nk_p, nv_p, nca_s, ncb_s)


def run(cfg, inputs):
    nc = build_program(cfg)
    in_maps, meta = make_in_maps(cfg, inputs)
    res = run_bass_kernel_spmd(nc, in_maps, core_ids=list(range(NCORES)))
    return assemble(cfg, res.results, meta)


def kernel(**inputs):
    return run(Cfg(), inputs)
```

```python
import numpy as np
import concourse.bass as bass
import concourse.mybir as mybir
from concourse.bass_utils import run_bass_kernel_spmd

F32 = mybir.dt.float32
BF16 = mybir.dt.bfloat16
AF = mybir.ActivationFunctionType
ALU = mybir.AluOpType

RMS_EPS = 1e-6
LN_EPS = 1e-5
NCORES = 8


class Cfg:
    def __init__(self, D=2048, DFF=5632, NMEM=256, NH=4, NP=4096, TW=464, KA=31, KB=3,
                 NS=32, HALO=32, NSLOT=14):
        self.D, self.DFF, self.NMEM, self.NH, self.NP, self.TW = D, DFF, NMEM, NH, NP, TW
        self.KA, self.KB, self.NS, self.HALO, self.NSLOT = KA, KB, NS, HALO, NSLOT
        self.DC = D // 2
        self.ND, self.NCC, self.NF, self.NM = D // 128, self.DC // 128, DFF // 128, NMEM // 128
        self.HD = D // NH
        self.HC = self.HD // 128
        self.DIN = 5 * self.DC + 2 * D
        self.NIN = self.DIN // 128
        n = self.NCC
        self.C_AV, self.C_AG, self.C_BB, self.C_BC, self.C_BX = 0, n, 2 * n, 3 * n, 4 * n
        self.C_GA, self.C_GB = 5 * n, 5 * n + self.ND
        self.PA, self.PB = KA - 1, KB - 1
        o = 0
        def take(k):
            nonlocal o
            r = o
            o += k
            return r
        self.V_NMIX = take(self.ND)
        self.V_BIN = take(self.NIN)
        self.V_CAW = take(KA * n)
        self.V_CAB = take(n)
        self.V_LNG = take(n)
        self.V_LNB = take(n)
        self.V_BAO = take(self.ND)
        self.V_CBW = take(KB * n)
        self.V_NX = take(self.ND)
        self.V_NMEM = take(self.ND)
        self.V_NFFN = take(self.ND)
        self.V_NFIN = take(self.ND)
        self.NV = o
        self.weights = [
            ("w_in", D, self.DIN), ("w_a_out", self.DC, D), ("w_b_out", self.DC, D),
            ("w_mix_out", D, D), ("w_q", D, D), ("w_k", D, D), ("w_v", D, D), ("w_o", D, D),
            ("w_gate", D, DFF), ("w_up", D, DFF), ("w_down", DFF, D),
        ]
        self.wtile = {}
        self.units = []
        off = 0
        for name, K, N in self.weights:
            nkc, njo = K // 128, N // 128
            kgs = [(k0, min(16, nkc - k0)) for k0 in range(0, nkc, 16)]
            for kg, (k0, nk) in enumerate(kgs):
                for jo0 in range(0, njo, 4):
                    nj = min(4, njo - jo0)
                    self.units.append((name, k0, nk, jo0, nj, off))
                    for j in range(nj):
                        self.wtile[(name, jo0 + j, kg)] = (off, nk, k0)
                        off += 128 * nk * 128
        self.WSCR = off
        self.nkg = {name: (K // 128 + 15) // 16 for name, K, N in self.weights}
        tot = HALO + NP
        nt = -(-tot // TW)
        base = -(-tot // nt)
        base = -(-base // 8) * 8
        widths = []
        rem = tot
        while rem > 0:
            w = min(base, rem)
            widths.append(w)
            rem -= w
        self.tiles = []
        r = 0
        for i, w in enumerate(widths):
            h = HALO if i == 0 else 0
            self.tiles.append(dict(W=w, row0=r, kind="prompt", outs=[(h, w - h, r + h - HALO)], halo=h))
            r += w
        self.tiles.append(dict(W=NS, row0=HALO + NP, kind="sample", outs=[(0, NS, NP)], halo=0))
        self.NROWS = HALO + NP + NS


class _Stop(Exception):
    pass


class Ev:
    __slots__ = ("sem", "val")

    def __init__(self, sem, val):
        self.sem, self.val = sem, val


class Buf:
    __slots__ = ("name", "w", "r", "lo", "hi", "al", "excl")

    def __init__(self, name, lo=None, n=None):
        self.name = name
        self.excl = False
        self.w = []
        self.r = {}
        self.lo = lo
        self.hi = None if lo is None else lo + n
        self.al = ()


def link_aliases(bufs):
    rb = [b for b in bufs if b.lo is not None]
    for b in rb:
        b.al = tuple(o for o in rb if o is not b and o.lo < b.hi and b.lo < o.hi)


class Prog:
    ENGS = ("pe", "act", "dve", "pool", "sp")

    def __init__(self):
        self.q = {e: [] for e in self.ENGS}
        self.cnt = {e: 0 for e in self.ENGS}
        self.known = {e: {} for e in self.ENGS}
        self.dcnt = {}

    def _waits(self, eng, reads, writes):
        need = {}
        def add(ev, src, raw):
            if src == eng and (eng == "pe" or not raw):
                return
            if need.get(ev.sem, 0) < ev.val:
                need[ev.sem] = ev.val
        for b in reads:
            for ev, src in b.w:
                add(ev, src, True)
            if b.excl:
                for ev, src in b.r.values():
                    add(ev, src, False)
        for b0 in writes:
            for b in (b0,) + tuple(b0.al):
                for ev, src in b.w:
                    add(ev, src, False)
                for ev, src in b.r.values():
                    add(ev, src, False)
        kn = self.known[eng]
        for sem, val in need.items():
            if kn.get(sem, 0) < val:
                kn[sem] = val
                self.q[eng].append(("wait", sem, val))

    def _commit(self, ev, src, reads, writes):
        for b0 in writes:
            for b in (b0,) + tuple(b0.al):
                b.w = [(ev, src)]
                b.r = {}
        for b in reads:
            old = b.r.get(ev.sem)
            if old is None or old[0].val < ev.val:
                b.r[ev.sem] = (ev, src)

    def op(self, eng, fn, reads=(), writes=(), sig=True):
        self._waits(eng, reads, writes)
        if sig:
            self.cnt[eng] += 1
            ev = Ev("c_" + eng, self.cnt[eng])
            self.q[eng].append(("op", fn, "c_" + eng, 1))
        else:
            ev = Ev("c_" + eng, self.cnt[eng] + 1)
            self.q[eng].append(("op", fn, None, 0))
        self._commit(ev, eng, reads, writes)
        return ev

    def dma(self, eng, sem, fn, reads=(), writes=()):
        self._waits(eng, reads, writes)
        self.dcnt[sem] = self.dcnt.get(sem, 0) + 16
        ev = Ev(sem, self.dcnt[sem])
        self.q[eng].append(("op", fn, sem, 16))
        self._commit(ev, "dma", reads, writes)
        return ev

    def wait_all(self, eng, sems):
        for s in sems:
            v = self.dcnt.get(s, 0)
            if v and self.known[eng].get(s, 0) < v:
                self.known[eng][s] = v
                self.q[eng].append(("wait", s, v))


def build_program(cfg):
    c = cfg
    nc = bass.Bass("TRN2", target_bir_lowering=False)
    D, DC, ND, NCC, NF, NM = c.D, c.DC, c.ND, c.NCC, c.NF, c.NM
    TWB = -(-max([t["W"] for t in c.tiles] + [c.NMEM]) // 16) * 16
    NXIN, NT32, AHEAD = 3, getattr(cfg, "NT32", 9), c.NSLOT - 3

    def din(name, shape, dt=F32):
        return nc.dram_tensor(name, list(shape), dt, kind="ExternalInput").ap()

    def dout(name, shape, dt=F32):
        return nc.dram_tensor(name, list(shape), dt, kind="ExternalOutput").ap()

    xall = din("xall", [c.NROWS, D])
    memb = din("memb", [c.NMEM, D])
    kc_in = din("kcache", [c.NMEM, D])
    vc_in = din("vcache", [c.NMEM, D])
    sa_in = din("sta", [c.PA, DC])
    sb_in = din("stb", [c.PB, DC])
    cv_in = din("cvec", [c.NV, 128])
    id_in = din("ident", [128, 128])
    mk_in = din("mask", [128, 1])
    w32 = din("w32", [c.WSCR])
    yall = dout("yall", [c.NP + c.NS, D])
    nca = dout("nca", [2, c.PA, DC])
    ncb = dout("ncb", [2, c.PB, DC])
    nk_o = dout("nk", [c.NMEM, D])
    nv_o = dout("nv", [c.NMEM, D])
    wscr = nc.dram_tensor("wscr", [c.WSCR], BF16, kind="Internal").ap()

    AW = c.PA + TWB
    CW = c.PB + TWB
    lay = {}
    o = 0

    def region(name, nbytes):
        nonlocal o
        lay[name] = o
        o += (nbytes + 31) // 32 * 32

    region("X", ND * TWB * 4)
    region("U", ND * TWB * 2)
    o_cx_rel = (NCC * AW * 4 + 31) // 32 * 32
    big = max(NF * TWB * 2,
              o_cx_rel + max(NCC * CW * 4, NCC * TWB * 2, 2 * D * 4),
              2 * ND * TWB * 2 + 4 * TWB * 2,
              2 * NM * D * 4)
    region("BIG", big)
    region("MID", max(NXIN * D * 4, NCC * TWB * 4 + NCC * TWB * 2))
    region("T32", NT32 * TWB * 4)
    region("TBF", 4 * TWB * 2)
    region("KT", ND * c.NMEM * 2)
    region("VS", NM * D * 2)
    region("WS", c.NSLOT * 16 * 128 * 2)
    region("CVEC", c.NV * 4)
    region("IDENT", 128 * 4)
    region("ONES", 128 * 2)
    region("CARA", NCC * c.PA * 4)
    region("CARB", NCC * c.PB * 4)
    region("MASK", 4)
    region("STST", DC * 4)
    region("CVST", 128 * 4)
    region("STG", 32)
    ARENA = o
    cfg.arena_bytes = o
    cfg.lay = dict(lay)

    P = Prog()
    sem_names = ["c_pe", "c_act", "c_dve", "c_pool", "c_sp", "cst", "cvl", "sti", "kvi0", "kvi1",
                 "sg0", "sg1", "xin0", "xin1", "xin2", "xin3", "yo0", "yo1",
                 "so", "kvo0", "kvo1"] + ["ws%d" % i for i in range(c.NSLOT)] + ["wst%d" % i for i in range(c.NSLOT)] + ["wc%d" % i for i in range(c.NSLOT)]

    import contextlib
    with contextlib.ExitStack() as es:
        arena = es.enter_context(nc.sbuf_tensor("arena", [128, ARENA // 4], F32))
        ps = es.enter_context(nc.psum_tensor("ps", [128, 8 * 512], F32))
        sems = {n: es.enter_context(nc.semaphore(n)) for n in sem_names}
        block = es.enter_context(nc.Block())

        def f32v(off, n):
            return arena[:, off // 4: off // 4 + n]

        def bfv(off, n):
            return arena[:, off // 4: off // 4 + n // 2].bitcast(BF16)

        allbufs = []

        def mk(name, lo=None, n=None):
            b = Buf(name, lo, n)
            allbufs.append(b)
            return b

        oX, oU, BIG, MID = lay["X"], lay["U"], lay["BIG"], lay["MID"]
        X = f32v(oX, ND * TWB).rearrange("p (c t) -> p c t", c=ND)
        bX = [mk("x%d" % i, oX + i * TWB * 4, TWB * 4) for i in range(ND)]
        U = bfv(oU, ND * TWB).rearrange("p (c t) -> p c t", c=ND)
        bU = [mk("u%d" % i, oU + i * TWB * 2, TWB * 2) for i in range(ND)]
        ABUF = f32v(BIG, NCC * AW).rearrange("p (c t) -> p c t", c=NCC)
        bA = [mk("a%d" % i, BIG + i * AW * 4, AW * 4) for i in range(NCC)]
        o_cx = BIG + o_cx_rel
        CXB = f32v(o_cx, NCC * CW).rearrange("p (c t) -> p c t", c=NCC)
        bCX = [mk("cx%d" % i, o_cx + i * CW * 4, CW * 4) for i in range(NCC)]
        AACT = bfv(o_cx, NCC * TWB).rearrange("p (c t) -> p c t", c=NCC)
        bAACT = [mk("aact%d" % i, o_cx + i * TWB * 2, TWB * 2) for i in range(NCC)]
        YOUT = [f32v(o_cx + i * D * 4, D) for i in range(2)]
        bYOUT = [mk("yout%d" % i, o_cx + i * D * 4, D * 4) for i in range(2)]
        QT = bfv(BIG, ND * TWB).rearrange("p (c t) -> p c t", c=ND)
        bQT = [mk("qt%d" % i, BIG + i * TWB * 2, TWB * 2) for i in range(ND)]
        oOT = BIG + ND * TWB * 2
        OT = bfv(oOT, ND * TWB).rearrange("p (c t) -> p c t", c=ND)
        bOT = [mk("ot%d" % i, oOT + i * TWB * 2, TWB * 2) for i in range(ND)]
        oPT = BIG + 2 * ND * TWB * 2
        PT = bfv(oPT, 4 * TWB).rearrange("p (a m t) -> p a m t", a=2, m=2)
        bPT = [mk("pt%d" % i, oPT + i * 2 * TWB * 2, 2 * TWB * 2) for i in range(2)]
        HFF = bfv(BIG, NF * TWB).rearrange("p (c t) -> p c t", c=NF)
        bHFF = [mk("hff%d" % i, BIG + i * TWB * 2, TWB * 2) for i in range(NF)]
        KVSTG = [f32v(BIG + i * NM * D * 4, NM * D).rearrange("p (b f) -> p b f", b=NM) for i in range(2)]
        bKVSTG = [mk("kvstg%d" % i, BIG + i * NM * D * 4, NM * D * 4) for i in range(2)]
        XIN = [f32v(MID + i * D * 4, D) for i in range(NXIN)]
        bXIN = [mk("xin%d" % i, MID + i * D * 4, D * 4) for i in range(NXIN)]
        YB = f32v(MID, NCC * TWB).rearrange("p (c t) -> p c t", c=NCC)
        bY = [mk("y%d" % i, MID + i * TWB * 4, TWB * 4) for i in range(NCC)]
        assert ND * TWB * 2 <= NCC * TWB * 4
        MB = bfv(MID, ND * TWB).rearrange("p (c t) -> p c t", c=ND)
        bM = [mk("m%d" % i, MID + i * TWB * 2, TWB * 2) for i in range(ND)]
        oBBM = MID + NCC * TWB * 4
        BBM = bfv(oBBM, NCC * TWB).rearrange("p (c t) -> p c t", c=NCC)
        bBBM = [mk("bbm%d" % i, oBBM + i * TWB * 2, TWB * 2) for i in range(NCC)]
        T32 = [f32v(lay["T32"] + i * TWB * 4, TWB) for i in range(NT32)]
        bT32 = [mk("t32_%d" % i, lay["T32"] + i * TWB * 4, TWB * 4) for i in range(NT32)]
        TBF = [bfv(lay["TBF"] + i * TWB * 2, TWB) for i in range(4)]
        bTBF = [mk("tbf_%d" % i, lay["TBF"] + i * TWB * 2, TWB * 2) for i in range(4)]
        KT = bfv(lay["KT"], ND * c.NMEM).rearrange("p (c m) -> p c m", c=ND)
        VS = bfv(lay["VS"], NM * D).rearrange("p (b f) -> p b f", b=NM)
        bKT, bVS = mk("kt", lay["KT"], ND * c.NMEM * 2), mk("vs", lay["VS"], NM * D * 2)
        WS = [bfv(lay["WS"] + i * 4096, 2048) for i in range(c.NSLOT)]
        bWS = [mk("ws%d" % i, lay["WS"] + i * 4096, 4096) for i in range(c.NSLOT)]
        CVEC = f32v(lay["CVEC"], c.NV)
        IDENT = f32v(lay["IDENT"], 128)
        ONES = bfv(lay["ONES"], 128)
        CARA = f32v(lay["CARA"], NCC * c.PA).rearrange("p (c t) -> p c t", c=NCC)
        CARB = f32v(lay["CARB"], NCC * c.PB).rearrange("p (c t) -> p c t", c=NCC)
        MASK = f32v(lay["MASK"], 1)
        STST = f32v(lay["STST"], DC)
        CVST = f32v(lay["CVST"], 128)
        bCVEC, bIDENT, bONES, bMASK = mk("cvec"), mk("ident"), mk("ones"), mk("mask")
        bCARA, bCARB, bSTST, bCVST = mk("cara"), mk("carb"), mk("stst"), mk("cvst")
        bSTG = []
        bSCR = {key: mk("scr_%s_%d_%d" % key) for key in c.wtile}
        bPS = [mk("ps%d" % i) for i in range(8)]
        for b_ in bPS:
            b_.excl = True
        link_aliases(allbufs)

        state = dict(bank=0, stats=0, t32=0, tbf=0, ws=0, xin=0, yo=0, ring6=1)

        def next_bank():
            n = 6 if state["ring6"] else 5
            i = state["bank"] % n
            state["bank"] = (i + 1) % n
            return i

        def next_stats():
            i = state["stats"]
            state["stats"] = 1 - i
            return 6 + i

        def tmp32():
            i = state["t32"]
            state["t32"] = (i + 1) % NT32
            return T32[i], bT32[i]

        def tmpbf():
            i = state["tbf"]
            state["tbf"] = (i + 1) % 4
            return TBF[i], bTBF[i]

        def cv(col):
            return CVEC[:, col:col + 1]

        def pb(bi, W, c0=0):
            return ps[:, bi * 512 + c0: bi * 512 + c0 + W]

        wseq = []
        wm = dict(dry=True, i=0, issued=0, pending=None, fu=0, first={}, count={})

        def wview(sl):
            return WS[sl].rearrange("p (k n) -> p k n", n=128)

        def w_store_pending():
            pend = wm["pending"]
            if pend is None:
                return
            wm["pending"] = None
            sl, key, off, n = pend
            dst = wscr[off: off + 128 * n].rearrange("(p f) -> p f", p=128)
            src = WS[sl][:, 0:n]
            P.dma("sp", "wst%d" % sl, lambda e, dst=dst, src=src: e.dma_start(out=dst, in_=src),
                  reads=[bWS[sl]], writes=[bSCR[key]])

        def w_issue(i):
            key = wseq[i]
            name, jo, kg = key
            off, nk, k0 = c.wtile[key]
            sl = i % c.NSLOT
            n = nk * 128
            if wm["first"][key] == i:
                src32 = w32[off: off + 128 * n].rearrange("(p f) -> p f", p=128)
                dst = WS[sl][:, 0:n]
                P.dma("pool", "wc%d" % sl, lambda e, dst=dst, src32=src32: e.dma_start(out=dst, in_=src32),
                      reads=[], writes=[bWS[sl]])
                w_store_pending()
                if wm["count"][key] > 1:
                    wm["pending"] = (sl, key, off, n)
            else:
                src = wscr[off: off + 128 * n].rearrange("(p f) -> p f", p=128)
                dst = WS[sl][:, 0:n]
                P.dma("sp", "ws%d" % sl, lambda e, dst=dst, src=src: e.dma_start(out=dst, in_=src),
                      reads=[bSCR[key]], writes=[bWS[sl]])
                w_store_pending()

        def wload(name, jo, kg=0):
            key = (name, jo, kg)
            nk = c.wtile[key][1]
            if wm["dry"]:
                wseq.append(key)
                return wview(0), bWS[0], nk
            i = wm["i"]
            wm["i"] += 1
            assert wseq[i] == key, (i, wseq[i], key)
            while wm["issued"] < min(len(wseq), i + 1 + AHEAD):
                w_issue(wm["issued"])
                wm["issued"] += 1
            sl = i % c.NSLOT
            return wview(sl), bWS[sl], nk

        def mm_group(bi, pairs, col0, ncol, all_sig=False):
            n = len(pairs)
            out = pb(bi, ncol, col0)
            for i, (l, r, rb) in enumerate(pairs):
                P.op("pe", lambda e, out=out, l=l, r=r, i=i: e.matmul(out, lhsT=l, rhs=r, start=(i == 0), stop=(i == n - 1)),
                     reads=rb, writes=[bPS[bi]], sig=(all_sig or i == n - 1))

        def proj_ws(name, jo, act, bact, W, nkg=1):
            bi = next_bank()
            pairs = []
            for kg in range(nkg):
                wt, bw, nk = wload(name, jo, kg)
                k0 = c.wtile[(name, jo, kg)][2]
                for k in range(nk):
                    pairs.append((wt[:, k, :], act[:, k0 + k, 0:W], [bw, bact[k0 + k]]))
            mm_group(bi, pairs, 0, W)
            return bi

        def rms_stat_mm(st, W, k, sq, bsq):
            P.op("pe", lambda e: e.matmul(pb(st, W), lhsT=ONES, rhs=sq[:, 0:W], start=(k == 0), stop=(k == ND - 1)),
                 reads=[bsq, bONES], writes=[bPS[st]], sig=True)

        def rms_square(W, k):
            sq, bsq = tmpbf()
            P.op("act", lambda e: e.activation(out=sq[:, 0:W], in_=X[:, k, 0:W], func=AF.Square),
                 reads=[bX[k]], writes=[bsq])
            return sq, bsq

        def rms_finish(st, W, gcol, to_x=False):
            sd, bsd = tmp32()
            P.op("act", lambda e: e.activation(out=sd[:, 0:W], in_=pb(st, W), func=AF.Ln, scale=1.0 / D, bias=RMS_EPS),
                 reads=[bPS[st]], writes=[bsd])
            P.op("act", lambda e: e.activation(out=pb(st, W), in_=sd[:, 0:W], func=AF.Exp, scale=-0.5),
                 reads=[bsd], writes=[bPS[st]])
            for k in range(ND):
                dst = X[:, k, 0:W] if to_x else U[:, k, 0:W]
                P.op("dve", lambda e, k=k, dst=dst: e.scalar_tensor_tensor(
                        out=dst, in0=X[:, k, 0:W], scalar=cv(gcol + k), in1=pb(st, W), op0=ALU.mult, op1=ALU.mult),
                     reads=[bX[k], bPS[st], bCVEC], writes=[bX[k]] if to_x else [bU[k]])

        def rmsnorm(W, gcol, to_x=False):
            st = next_stats()
            for k in range(ND):
                sq, bsq = rms_square(W, k)
                rms_stat_mm(st, W, k, sq, bsq)
            rms_finish(st, W, gcol, to_x)

        def residual_phase(name, act, bact, W, nkg=1):
            st = next_stats()
            pend = []
            for jo in range(ND):
                b = proj_ws(name, jo, act, bact, W, nkg=nkg)
                P.op("dve", lambda e, b=b, jo=jo: e.tensor_tensor(out=X[:, jo, 0:W], in0=pb(b, W), in1=X[:, jo, 0:W], op=ALU.add),
                     reads=[bPS[b], bX[jo]], writes=[bX[jo]])
                sq, bsq = rms_square(W, jo)
                pend.append((jo, sq, bsq))
                if len(pend) > 2:
                    k, sq0, bsq0 = pend.pop(0)
                    rms_stat_mm(st, W, k, sq0, bsq0)
            for k, sq0, bsq0 in pend:
                rms_stat_mm(st, W, k, sq0, bsq0)
            return st

        def xin_dma(src, row, n, eng="pool"):
            s = state["xin"]
            state["xin"] = (s + 1) % NXIN
            dst = XIN[s][0:n, :]
            sr = src[row: row + n, :]
            P.dma(eng, "xin%d" % s, lambda e, dst=dst, sr=sr: e.dma_start(out=dst, in_=sr),
                  reads=[], writes=[bXIN[s]])
            return s

        def load_rows(src, row0, W, eng="pool"):
            blocks, slots, deferred = [], [], []
            r = 0
            while r < W:
                n = min(128, W - r)
                if len(blocks) < NXIN:
                    slots.append(xin_dma(src, row0 + r, n, eng))
                    blocks.append((r, n))
                else:
                    deferred.append((r, n, row0 + r))
                r += n
            return blocks, slots, deferred, src

        def transpose_in(blocks, slots, deferred=(), src=None):
            blocks, slots, deferred = list(blocks), list(slots), list(deferred)
            i = 0
            while i < len(blocks):
                (r, n), s = blocks[i], slots[i]
                i += 1
                for g in range(0, ND, 4):
                    bi = next_bank()
                    ng = min(4, ND - g)
                    for q in range(ng):
                        k = g + q
                        P.op("pe", lambda e, bi=bi, q=q, k=k, s=s, n=n: e.transpose(
                                pb(bi, n, q * 128), XIN[s][0:n, k * 128:(k + 1) * 128], IDENT[0:n, 0:n]),
                             reads=[bXIN[s], bIDENT], writes=[bPS[bi]], sig=(q == ng - 1))
                    srcp = pb(bi, ng * 128).rearrange("p (q t) -> p q t", q=ng)[:, :, 0:n]
                    dst = X[:, g:g + ng, r:r + n]
                    P.op("act", lambda e, srcp=srcp, dst=dst: e.activation(out=dst, in_=srcp, func=AF.Identity),
                         reads=[bPS[bi]], writes=[bX[g + q] for q in range(ng)])
                if deferred:
                    r2, n2, row2 = deferred.pop(0)
                    slots.append(xin_dma(src, row2, n2))
                    blocks.append((r2, n2))

        def state_in(src_ap, npre, car, bcar):
            P.dma("pool", "sti", lambda e: e.dma_start(out=STST[0:npre, :], in_=src_ap), reads=[], writes=[bSTST])
            bi = next_bank()
            for k in range(NCC):
                P.op("pe", lambda e, k=k, bi=bi: e.transpose(pb(bi, npre, k * npre), STST[0:npre, k * 128:(k + 1) * 128],
                                                           IDENT[0:npre, 0:npre]),
                     reads=[bSTST, bIDENT], writes=[bPS[bi]], sig=(k == NCC - 1))
            P.op("act", lambda e, bi=bi: e.activation(
                    out=car, in_=pb(bi, NCC * npre).rearrange("p (c t) -> p c t", c=NCC), func=AF.Identity),
                 reads=[bPS[bi]], writes=[bcar])

        def state_out(car, bcar, npre, dst_ap):
            for g in range(0, NCC, 4):
                bi = next_bank()
                ng = min(4, NCC - g)
                for q in range(ng):
                    k = g + q
                    P.op("pe", lambda e, bi=bi, q=q, k=k: e.transpose(
                            ps[0:npre, bi * 512 + q * 128: bi * 512 + (q + 1) * 128], car[:, k, :], IDENT),
                         reads=[bcar, bIDENT], writes=[bPS[bi]], sig=(q == ng - 1))
                P.op("act", lambda e, bi=bi, g=g, ng=ng: e.activation(
                        out=STST[0:npre, g * 128:(g + ng) * 128], in_=ps[0:npre, bi * 512: bi * 512 + ng * 128],
                        func=AF.Identity),
                     reads=[bPS[bi]], writes=[bSTST])
            P.dma("pool", "so", lambda e: e.dma_start(out=dst_ap, in_=STST[0:npre, :]), reads=[bSTST], writes=[])

        def stage(k):
            if getattr(cfg, "stop", None) is not None and k > cfg.stop:
                raise _Stop()

        def program():
          if True:
            P.dma("sp", "cst", lambda e: e.dma_start(out=IDENT, in_=id_in[:, :]), reads=[], writes=[bIDENT])
            P.dma("sp", "cst", lambda e: e.dma_start(out=MASK, in_=mk_in[:, :]), reads=[], writes=[bMASK])
            tot = P.dcnt["cst"]
            bIDENT.w = [(Ev("cst", tot), "dma")]
            bMASK.w = [(Ev("cst", tot), "dma")]
            P.op("pool", lambda e: e.memset(ONES, 1.0), reads=[], writes=[bONES])
            r0 = 0
            while r0 < c.NV:
                n = min(128, c.NV - r0)
                P.dma("sp", "cvl", lambda e, r0=r0, n=n: e.dma_start(out=CVST[0:n, :], in_=cv_in[r0:r0 + n, :]),
                      reads=[], writes=[bCVST])
                bi = next_bank()
                P.op("pe", lambda e, bi=bi, n=n: e.transpose(pb(bi, n), CVST[0:n, :], IDENT[0:n, 0:n]),
                     reads=[bCVST, bIDENT], writes=[bPS[bi]])
                P.op("act", lambda e, bi=bi, n=n, r0=r0: e.activation(out=CVEC[:, r0:r0 + n], in_=pb(bi, n), func=AF.Identity),
                     reads=[bPS[bi]], writes=[bCVEC])
                r0 += n

            scale_att = float(c.HD) ** -0.5

            def mixer(W, tile):
                halo = tile["halo"]
                sb = 4 + 10 * tile.get("ti", 0)
                if halo:
                    for j in range(NCC):
                        P.op("dve", lambda e, j=j: e.memset(ABUF[:, j, 0:c.PA], 0.0), reads=[], writes=[bA[j]])
                        P.op("dve", lambda e, j=j: e.memset(CXB[:, j, 0:c.PB], 0.0), reads=[], writes=[bCX[j]])
                else:
                    P.op("pool", lambda e: e.tensor_copy(out=ABUF[:, :, 0:c.PA], in_=CARA), reads=[bCARA], writes=bA)
                    P.op("pool", lambda e: e.tensor_copy(out=CXB[:, :, 0:c.PB], in_=CARB), reads=[bCARB], writes=bCX)
                stage(sb + 0.1)
                rmsnorm(W, c.V_NMIX)
                stage(sb + 0.2)
                s1, s2 = 6, 7
                state["ring6"] = 0
                for j in range(NCC):
                    bv = proj_ws("w_in", c.C_AV + j, U, bU, W)
                    bg = proj_ws("w_in", c.C_AG + j, U, bU, W)
                    sg, bsg = tmp32()
                    P.op("act", lambda e, bg=bg, sg=sg, j=j: e.activation(out=sg[:, 0:W], in_=pb(bg, W), func=AF.Sigmoid,
                                                                          bias=cv(c.V_BIN + c.C_AG + j)),
                         reads=[bPS[bg], bCVEC], writes=[bsg])
                    P.op("dve", lambda e, bv=bv, sg=sg, j=j: e.scalar_tensor_tensor(
                            out=ABUF[:, j, c.PA:c.PA + W], in0=pb(bv, W), scalar=cv(c.V_BIN + c.C_AV + j),
                            in1=sg[:, 0:W], op0=ALU.add, op1=ALU.mult),
                         reads=[bPS[bv], bsg, bCVEC], writes=[bA[j]])
                    if halo:
                        P.op("dve", lambda e, j=j: e.tensor_scalar(out=ABUF[:, j, c.PA:c.PA + halo], in0=ABUF[:, j, c.PA:c.PA + halo],
                                                                   scalar1=MASK, scalar2=None, op0=ALU.mult),
                             reads=[bA[j], bMASK], writes=[bA[j]])
                    bc_ = proj_ws("w_in", c.C_BC + j, U, bU, W)
                    bx_ = proj_ws("w_in", c.C_BX + j, U, bU, W)
                    bb_ = proj_ws("w_in", c.C_BB + j, U, bU, W)
                    xs, bxs = tmp32()
                    P.op("act", lambda e, bx_=bx_, xs=xs, j=j: e.activation(out=xs[:, 0:W], in_=pb(bx_, W), func=AF.Identity,
                                                                            bias=cv(c.V_BIN + c.C_BX + j)),
                         reads=[bPS[bx_], bCVEC], writes=[bxs])
                    cs, bcs = tmp32()
                    P.op("act", lambda e, bc_=bc_, cs=cs, j=j: e.activation(out=cs[:, 0:W], in_=pb(bc_, W), func=AF.Identity,
                                                                            bias=cv(c.V_BIN + c.C_BC + j)),
                         reads=[bPS[bc_], bCVEC], writes=[bcs])
                    P.op("pool", lambda e, cs=cs, xs=xs, j=j: e.tensor_tensor(out=CXB[:, j, c.PB:c.PB + W], in0=cs[:, 0:W],
                                                                              in1=xs[:, 0:W], op=ALU.mult),
                         reads=[bcs, bxs], writes=[bCX[j]])
                    if halo:
                        P.op("dve", lambda e, j=j: e.tensor_scalar(out=CXB[:, j, c.PB:c.PB + halo], in0=CXB[:, j, c.PB:c.PB + halo],
                                                                   scalar1=MASK, scalar2=None, op0=ALU.mult),
                             reads=[bCX[j], bMASK], writes=[bCX[j]])
                    cb, bcb = tmp32()
                    P.op("pool", lambda e, cb=cb, j=j: e.tensor_scalar(out=cb[:, 0:W], in0=CXB[:, j, 0:W],
                                                                       scalar1=cv(c.V_CBW + j), scalar2=None, op0=ALU.mult),
                         reads=[bCX[j], bCVEC], writes=[bcb])
                    for t in range(1, c.KB):
                        tt, btt = tmp32()
                        P.op("pool", lambda e, tt=tt, j=j, t=t: e.tensor_scalar(out=tt[:, 0:W], in0=CXB[:, j, t:t + W],
                                                                                scalar1=cv(c.V_CBW + t * NCC + j), scalar2=None, op0=ALU.mult),
                             reads=[bCX[j], bCVEC], writes=[btt])
                        P.op("pool", lambda e, tt=tt, cb=cb: e.tensor_tensor(out=cb[:, 0:W], in0=cb[:, 0:W], in1=tt[:, 0:W], op=ALU.add),
                             reads=[bcb, btt], writes=[bcb])
                    bs, bbs = tmp32()
                    P.op("act", lambda e, bb_=bb_, bs=bs, j=j: e.activation(out=bs[:, 0:W], in_=pb(bb_, W), func=AF.Identity,
                                                                            bias=cv(c.V_BIN + c.C_BB + j)),
                         reads=[bPS[bb_], bCVEC], writes=[bbs])
                    P.op("pool", lambda e, bs=bs, cb=cb, j=j: e.tensor_tensor(out=BBM[:, j, 0:W], in0=bs[:, 0:W], in1=cb[:, 0:W], op=ALU.mult),
                         reads=[bbs, bcb], writes=[bBBM[j]])
                    acc = pb(5, W)
                    P.op("dve", lambda e, j=j, acc=acc: e.tensor_scalar(out=acc, in0=ABUF[:, j, 0:W], scalar1=cv(c.V_CAW + j),
                                                                        scalar2=cv(c.V_CAB + j), op0=ALU.mult, op1=ALU.add),
                         reads=[bA[j], bCVEC], writes=[bPS[5]])
                    for t in range(1, c.KA):
                        P.op("dve", lambda e, j=j, t=t, acc=acc: e.scalar_tensor_tensor(
                                out=acc, in0=ABUF[:, j, t:t + W], scalar=cv(c.V_CAW + t * NCC + j), in1=acc,
                                op0=ALU.mult, op1=ALU.add),
                             reads=[bA[j], bPS[5], bCVEC], writes=[bPS[5]])
                    P.op("act", lambda e, j=j, acc=acc: e.activation(out=YB[:, j, 0:W], in_=acc, func=AF.Identity),
                         reads=[bPS[5]], writes=[bY[j]])
                    yb, byb = tmpbf()
                    P.op("act", lambda e, yb=yb, acc=acc: e.activation(out=yb[:, 0:W], in_=acc, func=AF.Identity),
                         reads=[bPS[5]], writes=[byb])
                    sq, bsq = tmpbf()
                    P.op("act", lambda e, sq=sq, acc=acc: e.activation(out=sq[:, 0:W], in_=acc, func=AF.Square),
                         reads=[bPS[5]], writes=[bsq])
                    if j >= 1:
                        pend()
                    def pend(yb=yb, sq=sq, byb=byb, bsq=bsq, j=j):
                        P.op("pe", lambda e: e.matmul(pb(s1, W), lhsT=ONES, rhs=yb[:, 0:W], start=(j == 0), stop=(j == NCC - 1)),
                             reads=[byb, bONES], writes=[bPS[s1]], sig=True)
                        P.op("pe", lambda e: e.matmul(pb(s2, W), lhsT=ONES, rhs=sq[:, 0:W], start=(j == 0), stop=(j == NCC - 1)),
                             reads=[bsq, bONES], writes=[bPS[s2]], sig=True)
                pend()
                state["ring6"] = 1
                P.op("pool", lambda e: e.tensor_copy(out=CARA, in_=ABUF[:, :, W:W + c.PA]), reads=bA, writes=[bCARA])
                P.op("pool", lambda e: e.tensor_copy(out=CARB, in_=CXB[:, :, W:W + c.PB]), reads=bCX, writes=[bCARB])
                stage(sb + 0.6)
                mean, bmean = tmp32()
                msq, bmsq = tmp32()
                var, bvar = tmp32()
                P.op("dve", lambda e: e.tensor_scalar(out=mean[:, 0:W], in0=pb(s1, W), scalar1=1.0 / DC, scalar2=None, op0=ALU.mult),
                     reads=[bPS[s1]], writes=[bmean])
                stage(sb + 0.61)
                P.op("act", lambda e: e.activation(out=msq[:, 0:W], in_=pb(s1, W), func=AF.Square, scale=1.0 / DC),
                     reads=[bPS[s1]], writes=[bmsq])
                stage(sb + 0.62)
                P.op("dve", lambda e: e.scalar_tensor_tensor(out=var[:, 0:W], in0=pb(s2, W), scalar=1.0 / DC, in1=msq[:, 0:W],
                                                             op0=ALU.mult, op1=ALU.subtract),
                     reads=[bPS[s2], bmsq], writes=[bvar])
                stage(sb + 0.63)
                P.op("act", lambda e: e.activation(out=var[:, 0:W], in_=var[:, 0:W], func=AF.Ln, bias=LN_EPS),
                     reads=[bvar], writes=[bvar])
                stage(sb + 0.64)
                P.op("act", lambda e: e.activation(out=pb(s2, W), in_=var[:, 0:W], func=AF.Exp, scale=-0.5),
                     reads=[bvar], writes=[bPS[s2]])
                stage(sb + 0.65)
                for j in range(NCC):
                    t_, bt_ = (msq, bmsq) if j % 2 == 0 else (var, bvar)
                    P.op("pool" if j % 2 == 0 else "dve",
                         lambda e, j=j, t_=t_: e.tensor_tensor(out=t_[:, 0:W], in0=YB[:, j, 0:W], in1=mean[:, 0:W],
                                                               op=ALU.subtract),
                         reads=[bY[j], bmean], writes=[bt_])
                    stage(sb + 0.66)
                    P.op("dve", lambda e, t_=t_: e.tensor_tensor(out=t_[:, 0:W], in0=pb(s2, W), in1=t_[:, 0:W], op=ALU.mult),
                         reads=[bPS[s2], bt_], writes=[bt_])
                    stage(sb + 0.67)
                    P.op("act", lambda e, j=j, t_=t_: e.activation(out=AACT[:, j, 0:W], in_=t_[:, 0:W], func=AF.Silu,
                                                                   scale=cv(c.V_LNG + j), bias=cv(c.V_LNB + j)),
                         reads=[bt_, bCVEC], writes=[bAACT[j]])
                stage(sb + 0.7)
                def gates(jo):
                    b2 = proj_ws("w_in", c.C_GA + jo, U, bU, W)
                    b4 = proj_ws("w_in", c.C_GB + jo, U, bU, W)
                    b3 = proj_ws("w_b_out", jo, BBM, bBBM, W)
                    sA, bsA = tmp32()
                    sB, bsB = tmp32()
                    P.op("act", lambda e: e.activation(out=sA[:, 0:W], in_=pb(b2, W), func=AF.Sigmoid,
                                                       bias=cv(c.V_BIN + c.C_GA + jo)),
                         reads=[bPS[b2], bCVEC], writes=[bsA])
                    P.op("act", lambda e: e.activation(out=sB[:, 0:W], in_=pb(b4, W), func=AF.Sigmoid,
                                                       bias=cv(c.V_BIN + c.C_GB + jo)),
                         reads=[bPS[b4], bCVEC], writes=[bsB])
                    P.op("dve", lambda e: e.tensor_tensor(out=sB[:, 0:W], in0=pb(b3, W), in1=sB[:, 0:W], op=ALU.mult),
                         reads=[bPS[b3], bsB], writes=[bsB])
                    return sA, bsA, sB, bsB

                cur = gates(0)
                for jo in range(ND):
                    nxt = gates(jo + 1) if jo + 1 < ND else None
                    b1 = proj_ws("w_a_out", jo, AACT, bAACT, W)
                    sA, bsA, sB, bsB = cur
                    P.op("dve", lambda e, b1=b1, sA=sA, jo=jo: e.scalar_tensor_tensor(
                            out=sA[:, 0:W], in0=pb(b1, W), scalar=cv(c.V_BAO + jo), in1=sA[:, 0:W], op0=ALU.add, op1=ALU.mult),
                         reads=[bPS[b1], bsA, bCVEC], writes=[bsA])
                    P.op("pool", lambda e, sA=sA, sB=sB, jo=jo: e.tensor_tensor(out=MB[:, jo, 0:W], in0=sA[:, 0:W], in1=sB[:, 0:W],
                                                                                op=ALU.add),
                         reads=[bsA, bsB], writes=[bM[jo]])
                    cur = nxt
                stage(sb + 0.8)
                return residual_phase("w_mix_out", MB, bM, W)

            def attention(W, st_in):
                rms_finish(st_in, W, c.V_NX)
                for jo in range(ND):
                    b = proj_ws("w_q", jo, U, bU, W)
                    P.op("act", lambda e, b=b, jo=jo: e.activation(out=QT[:, jo, 0:W], in_=pb(b, W), func=AF.Identity),
                         reads=[bPS[b]], writes=[bQT[jo]])
                def scores(h):
                    par = h % 2
                    for mb in range(NM):
                        bi = next_bank()
                        pairs = [(KT[:, h * c.HC + dc, mb * 128:(mb + 1) * 128], QT[:, h * c.HC + dc, 0:W], [bKT, bQT[h * c.HC + dc]])
                                 for dc in range(c.HC)]
                        mm_group(bi, pairs, 0, W)
                        P.op("act", lambda e, bi=bi, par=par, mb=mb: e.activation(out=PT[:, par, mb, 0:W], in_=pb(bi, W),
                                                                                  func=AF.Exp, scale=scale_att),
                             reads=[bPS[bi]], writes=[bPT[par]])

                scores(0)
                for h in range(c.NH):
                    par = h % 2
                    if h + 1 < c.NH:
                        scores(h + 1)
                    st = next_stats()
                    mm_group(st, [(ONES, PT[:, par, mb, 0:W], [bPT[par], bONES]) for mb in range(NM)], 0, W)
                    rd, brd = tmp32()
                    P.op("dve", lambda e, st=st, rd=rd: e.reciprocal(out=rd[:, 0:W], in_=pb(st, W)), reads=[bPS[st]], writes=[brd])
                    for dc in range(c.HC):
                        bi = next_bank()
                        col = h * c.HD + dc * 128
                        pairs = [(VS[:, mb, col:col + 128], PT[:, par, mb, 0:W], [bVS, bPT[par]]) for mb in range(NM)]
                        mm_group(bi, pairs, 0, W)
                        P.op("dve", lambda e, bi=bi, rd=rd, k=h * c.HC + dc: e.tensor_tensor(
                                out=OT[:, k, 0:W], in0=pb(bi, W), in1=rd[:, 0:W], op=ALU.mult),
                             reads=[bPS[bi], brd], writes=[bOT[h * c.HC + dc]])
                return residual_phase("w_o", OT, bOT, W)

            def ffn(W, st_in):
                rms_finish(st_in, W, c.V_NFFN)
                for f in range(NF):
                    bg = proj_ws("w_gate", f, U, bU, W)
                    bu = proj_ws("w_up", f, U, bU, W)
                    sg, bsg = tmp32()
                    P.op("act", lambda e, bg=bg, sg=sg: e.activation(out=sg[:, 0:W], in_=pb(bg, W), func=AF.Silu),
                         reads=[bPS[bg]], writes=[bsg])
                    P.op("dve", lambda e, bu=bu, sg=sg, f=f: e.tensor_tensor(out=HFF[:, f, 0:W], in0=pb(bu, W), in1=sg[:, 0:W], op=ALU.mult),
                         reads=[bPS[bu], bsg], writes=[bHFF[f]])
                return residual_phase("w_down", HFF, bHFF, W, nkg=c.nkg["w_down"])

            def final_out(W, tile, st_in):
                rms_finish(st_in, W, c.V_NFIN, to_x=True)
                for (c0, n_out, orow) in tile["outs"]:
                    r = 0
                    while r < n_out:
                        n = min(128, n_out - r)
                        s = state["yo"]
                        state["yo"] = 1 - s
                        for g in range(0, ND, 4):
                            bi = next_bank()
                            ng = min(4, ND - g)
                            for q in range(ng):
                                k = g + q
                                P.op("pe", lambda e, bi=bi, q=q, k=k, n=n, cc=c0 + r: e.transpose(
                                        ps[0:n, bi * 512 + q * 128: bi * 512 + (q + 1) * 128], X[:, k, cc:cc + n], IDENT),
                                     reads=[bX[k], bIDENT], writes=[bPS[bi]], sig=(q == ng - 1))
                            P.op("act", lambda e, bi=bi, g=g, ng=ng, n=n, s=s: e.activation(
                                    out=YOUT[s][0:n, g * 128:(g + ng) * 128], in_=ps[0:n, bi * 512: bi * 512 + ng * 128], func=AF.Identity),
                                 reads=[bPS[bi]], writes=[bYOUT[s]])
                        dst = yall[orow + r: orow + r + n, :]
                        P.dma("pool", "yo%d" % s, lambda e, dst=dst, s=s, n=n: e.dma_start(out=dst, in_=YOUT[s][0:n, :]),
                              reads=[bYOUT[s]], writes=[])
                        r += n

            def kv_from_cache():
                for i, src in enumerate((kc_in, vc_in)):
                    P.dma("pool", "kvi%d" % i, lambda e, i=i, src=src: e.dma_start(
                            out=KVSTG[i], in_=src.rearrange("(b p) f -> p b f", p=128)),
                          reads=[], writes=[bKVSTG[i]])
                per = 512 // (NM * 128)
                for k in range(0, ND, per):
                    bi = next_bank()
                    nq = min(per, ND - k)
                    for q in range(nq):
                        for mb in range(NM):
                            P.op("pe", lambda e, bi=bi, q=q, mb=mb, k=k: e.transpose(
                                    pb(bi, 128, (q * NM + mb) * 128), KVSTG[0][:, mb, (k + q) * 128:(k + q + 1) * 128], IDENT),
                                 reads=[bKVSTG[0], bIDENT], writes=[bPS[bi]], sig=(q == nq - 1 and mb == NM - 1))
                    P.op("act", lambda e, bi=bi, k=k, nq=nq: e.activation(
                            out=KT[:, k:k + nq, :], in_=pb(bi, nq * NM * 128).rearrange("p (q m) -> p q m", q=nq), func=AF.Identity),
                         reads=[bPS[bi]], writes=[bKT])
                for mb in range(NM):
                    P.op("act", lambda e, mb=mb: e.activation(out=VS[:, mb, :], in_=KVSTG[1][:, mb, :], func=AF.Identity),
                         reads=[bKVSTG[1]], writes=[bVS])

            def kv_from_mem():
                W = c.NMEM
                transpose_in(*load_rows(memb, 0, W))
                rmsnorm(W, c.V_NMEM)
                stage(9.1)
                for wi, (wname, dst_o) in enumerate((("w_k", nk_o), ("w_v", nv_o))):
                    stage(9.2 + 0.4 * wi)
                    for jo in range(ND):
                        wt, bw, nk = wload(wname, jo, 0)
                        if wi == 0:
                            bi = next_bank()
                            mm_group(bi, [(wt[:, k, :], U[:, k, 0:W], [bw, bU[k]]) for k in range(ND)], 0, W)
                            P.op("act", lambda e, bi=bi, jo=jo: e.activation(out=KT[:, jo, :], in_=pb(bi, W), func=AF.Identity),
                                 reads=[bPS[bi]], writes=[bKT])
                        bi = next_bank()
                        for mb in range(NM):
                            mm_group(bi, [(U[:, k, mb * 128:(mb + 1) * 128], wt[:, k, :], [bw, bU[k]]) for k in range(ND)], mb * 128, 128)
                        src = pb(bi, NM * 128).rearrange("p (b n) -> p b n", b=NM)
                        P.op("act", lambda e, src=src, wi=wi, jo=jo: e.activation(out=KVSTG[wi][:, :, jo * 128:(jo + 1) * 128], in_=src,
                                                                                  func=AF.Identity),
                             reads=[bPS[bi]], writes=[bKVSTG[wi]])
                        if wi == 1:
                            P.op("act", lambda e, src=src, jo=jo: e.activation(out=VS[:, :, jo * 128:(jo + 1) * 128], in_=src,
                                                                               func=AF.Identity),
                                 reads=[bPS[bi]], writes=[bVS])
                    stage(9.3 + 0.4 * wi)
                    P.dma("pool", "kvo%d" % wi, lambda e, wi=wi, dst_o=dst_o: e.dma_start(
                            out=dst_o.rearrange("(b p) f -> p b f", p=128), in_=KVSTG[wi]),
                          reads=[bKVSTG[wi]], writes=[])

            stage(2)
            tiles = c.tiles
            kv_from_mem()
            pre = load_rows(xall, tiles[0]["row0"], tiles[0]["W"])
            for ti, tile in enumerate(tiles):
                W = tile["W"]
                tile["ti"] = ti
                if tile["kind"] == "sample":
                    state_out(CARA, bCARA, c.PA, nca[0])
                    state_out(CARB, bCARB, c.PB, ncb[0])
                    state_in(sa_in[:, :], c.PA, CARA, bCARA)
                    state_in(sb_in[:, :], c.PB, CARB, bCARB)
                stage(3 + 10 * ti)
                transpose_in(*pre)
                stage(4 + 10 * ti)
                st_ = mixer(W, tile)
                stage(5 + 10 * ti)
                if ti + 1 < len(tiles):
                    nt_ = tiles[ti + 1]
                    pre = load_rows(xall, nt_["row0"], nt_["W"])
                stage(6 + 10 * ti)
                st_ = attention(W, st_)
                if ti + 1 < len(tiles) and tiles[ti + 1]["kind"] == "sample":
                    kv_from_cache()
                stage(7 + 10 * ti)
                st_ = ffn(W, st_)
                stage(8 + 10 * ti)
                final_out(W, tile, st_)
                stage(9 + 10 * ti)
            state_out(CARA, bCARA, c.PA, nca[1])
            state_out(CARB, bCARB, c.PB, ncb[1])
            w_store_pending()

        def reset_tracking():
            P.__init__()
            for b_ in allbufs:
                b_.w = []
                b_.r = {}
            for k_ in state:
                state[k_] = 0
            state["ring6"] = 1

        saved_stop = getattr(cfg, "stop", None)
        cfg.stop = None
        program()
        cfg.stop = saved_stop
        for i_, key_ in enumerate(wseq):
            wm["first"].setdefault(key_, i_)
            wm["count"][key_] = wm["count"].get(key_, 0) + 1
        wm["dry"] = False
        reset_tracking()
        try:
            program()
        except _Stop:
            pass
        P.wait_all("pool", ["yo0", "yo1", "so", "kvo0", "kvo1"])
        P.wait_all("sp", ["cst", "cvl", "sg0", "sg1"] + ["ws%d" % i for i in range(c.NSLOT)] + ["wst%d" % i for i in range(c.NSLOT)])
        P.wait_all("pool", ["sti", "kvi0", "kvi1", "xin0", "xin1", "xin2"] + ["wc%d" % i for i in range(c.NSLOT)])

        def replay(eng_name):
            def run(e):
                for item in P.q[eng_name]:
                    if item[0] == "wait":
                        e.wait_ge(sems[item[1]], item[2])
                    else:
                        ins = item[1](e)
                        if item[2] is not None:
                            ins.then_inc(sems[item[2]], item[3])
            return run

        block.sync(replay("sp"))
        block.tensor(replay("pe"))
        block.scalar(replay("act"))
        block.vector(replay("dve"))
        block.gpsimd(replay("pool"))
        cfg.stats = {e: len(P.q[e]) for e in P.ENGS}
    return nc


def make_in_maps(cfg, inputs):
    c = cfg
    f = lambda a: np.ascontiguousarray(np.asarray(a, dtype=np.float32))
    xp = f(inputs["x_prompt"])
    xs = f(inputs["x_sample"])
    B, S, D = xp.shape
    segs = S // c.NP
    vec_names = ["norm_mix_g", "b_in", "conv_a_w", "conv_a_b", "ln_a_g", "ln_a_b", "b_a_out", "conv_b_w",
                 "norm_x_g", "norm_mem_g", "norm_ffn_g", "norm_final_g"]
    cvec = np.concatenate([f(inputs[n]).reshape(-1, 128) for n in vec_names], axis=0)
    assert cvec.shape[0] == c.NV, (cvec.shape, c.NV)
    ident = np.eye(128, dtype=np.float32)
    w32 = np.empty((c.WSCR,), np.float32)
    for name, K, N in c.weights:
        Wm = f(inputs[name])[0]
        nkc, njo = K // 128, N // 128
        W4 = Wm.reshape(nkc, 128, njo, 128)
        for (nm, jo, kg), (off, nk, k0) in c.wtile.items():
            if nm != name:
                continue
            t_ = W4[k0:k0 + nk, :, jo, :].transpose(1, 0, 2)
            w32[off: off + 128 * nk * 128] = t_.reshape(-1)
    in_maps = []
    for core in range(NCORES):
        b, s = core // segs, core % segs
        t0 = s * c.NP
        xall = np.empty((c.NROWS, D), np.float32)
        if s == 0:
            xall[0:c.HALO] = 0.0
        else:
            xall[0:c.HALO] = xp[b, t0 - c.HALO:t0]
        xall[c.HALO:c.HALO + c.NP] = xp[b, t0:t0 + c.NP]
        xall[c.HALO + c.NP:] = xs[core]
        m = dict(
            xall=xall,
            memb=f(inputs["mem_prompt"])[b],
            kcache=f(inputs["cache_mem_k"])[0, core].reshape(c.NMEM, D),
            vcache=f(inputs["cache_mem_v"])[0, core].reshape(c.NMEM, D),
            sta=f(inputs["state_conv_a"])[0, core],
            stb=f(inputs["state_conv_b"])[0, core],
            cvec=cvec,
            ident=ident,
            mask=np.full((128, 1), 0.0 if s == 0 else 1.0, np.float32),
        )
        m["w32"] = w32
        in_maps.append(m)
    return in_maps, (B, S, D, segs)


def assemble(cfg, res, meta):
    c = cfg
    B, S, D, segs = meta
    y_prompt = np.empty((B, S, D), np.float32)
    y_sample = np.empty((NCORES, c.NS, D), np.float32)
    nca_p = np.empty((1, B, c.PA, c.DC), np.float32)
    ncb_p = np.empty((1, B, c.PB, c.DC), np.float32)
    nk_p = np.empty((1, B, c.NMEM, c.NH, c.HD), np.float32)
    nv_p = np.empty((1, B, c.NMEM, c.NH, c.HD), np.float32)
    nca_s = np.empty((1, NCORES, c.PA, c.DC), np.float32)
    ncb_s = np.empty((1, NCORES, c.PB, c.DC), np.float32)
    for core in range(NCORES):
        r = res[core]
        b, s = core // segs, core % segs
        y_prompt[b, s * c.NP:(s + 1) * c.NP] = r["yall"][0:c.NP]
        y_sample[core] = r["yall"][c.NP:]
        nca_s[0, core] = r["nca"][1]
        ncb_s[0, core] = r["ncb"][1]
        if s == segs - 1:
            nca_p[0, b] = r["nca"][0]
            ncb_p[0, b] = r["ncb"][0]
        if s == 0:
            nk_p[0, b] = r["nk"].reshape(c.NMEM, c.NH, c.HD)
            nv_p[0, b] = r["nv"].reshape(c.NMEM, c.NH, c.HD)
    return (y_prompt, y_sample, nca_p, ncb_p, nk_p, nv_p, nca_s, ncb_s)


def run(cfg, inputs):
    nc = build_program(cfg)
    in_maps, meta = make_in_maps(cfg, inputs)
    res = run_bass_kernel_spmd(nc, in_maps, core_ids=list(range(NCORES)))
    return assemble(cfg, res.results, meta)


def kernel(**inputs):
    return run(Cfg(), inputs)
```
